# Optimizing a Trainium2 kernel written in Bass

```python
import math
import jax, jax.numpy as jnp
from jax import lax
import numpy as np


D_MODEL = 1024
BATCH = 16
SEQ = 2048
DEPTH = 1

HEAD_DIM = 64
FOX_HEADS = 12
DIL_HEADS = 12
MEM_HEADS = 4
MEM_HEAD_DIM = 128
MEM_LEN = 256
FOX_W = FOX_HEADS * HEAD_DIM
DIL_W = DIL_HEADS * HEAD_DIM
MEM_W = MEM_HEADS * MEM_HEAD_DIM
MIX_W = FOX_W + DIL_W + MEM_W
DILATIONS = ((128, 1), (512, 4), (2048, 16))
BLOCK = 128
ROPE_THETA = 500000.0
ROPE_DIM = HEAD_DIM // 4
RMS_EPS = 1e-6
NEG_INF = -1e30
IN_SIZES = [FOX_W] * 4 + [FOX_HEADS] + [DIL_W] * 4 + [MEM_W] * 2
IN_W = sum(IN_SIZES)

kernel_name = 'hymba_fox_dilated_memory_block'


def rmsnorm(x, g):
    xf = x.astype(jnp.float32)
    y = xf * lax.rsqrt(jnp.mean(xf * xf, axis=-1, keepdims=True) + RMS_EPS)
    return (y * g.astype(jnp.float32)).astype(x.dtype)


def rope_partial(t, pos):
    half = ROPE_DIM // 2
    inv_freq = 1.0 / (ROPE_THETA ** (jnp.arange(0, ROPE_DIM, 2, dtype=jnp.float32) / ROPE_DIM))
    ang = pos[:, None] * inv_freq[None, :]
    cos = jnp.cos(ang)[None, :, None, :]
    sin = jnp.sin(ang)[None, :, None, :]
    tr = t[..., :ROPE_DIM].astype(jnp.float32)
    t1, t2 = tr[..., :half], tr[..., half:]
    rot = jnp.concatenate([t1 * cos - t2 * sin, t2 * cos + t1 * sin], axis=-1)
    return jnp.concatenate([rot.astype(t.dtype), t[..., ROPE_DIM:]], axis=-1)


def forgetting_attention(q, k, v, logf):
    B, S, H, E = q.shape
    scale = 1.0 / math.sqrt(E)
    c = jnp.cumsum(logf, axis=1).transpose(0, 2, 1)
    vf = v.astype(jnp.float32)
    outs = []
    for i in range(S // BLOCK):
        q0, q1 = i * BLOCK, (i + 1) * BLOCK
        s = jnp.einsum('bqhe,bkhe->bhqk', q[:, q0:q1], k[:, :q1]).astype(jnp.float32) * scale
        s = s + c[:, :, q0:q1, None] - c[:, :, None, :q1]
        mask = (q0 + jnp.arange(BLOCK))[:, None] >= jnp.arange(q1)[None, :]
        s = jnp.where(mask[None, None], s, NEG_INF)
        p = jax.nn.softmax(s, axis=-1)
        outs.append(jnp.einsum('bhqk,bkhe->bqhe', p, vf[:, :q1]))
    return jnp.concatenate(outs, axis=1)


def dilated_pattern(q, k, v, dilation, n_steps):
    B, S, H, E = q.shape
    L = S // dilation
    nb = -(-L // BLOCK)
    Lp = nb * BLOCK
    scale = 1.0 / math.sqrt(E)

    def to_blocks(t):
        t = t.reshape(B, L, dilation, H, E)
        t = jnp.pad(t, ((0, 0), (0, Lp - L), (0, 0), (0, 0), (0, 0)))
        return t.reshape(B, nb, BLOCK, dilation, H, E)

    def with_prev(t):
        prev = jnp.pad(t, ((0, 0), (1, 0), (0, 0), (0, 0), (0, 0), (0, 0)))[:, :nb]
        return jnp.concatenate([prev, t], axis=2)

    qb = to_blocks(q)
    kc = with_prev(to_blocks(k))
    vc = with_prev(to_blocks(v)).astype(jnp.float32)
    s = jnp.einsum('bnqrhe,bnkrhe->bnrhqk', qb, kc).astype(jnp.float32) * scale
    lq = jnp.arange(nb)[:, None] * BLOCK + jnp.arange(BLOCK)[None, :]
    lk = (jnp.arange(nb)[:, None] - 1) * BLOCK + jnp.arange(2 * BLOCK)[None, :]
    delta = lq[:, :, None] - lk[:, None, :]
    mask = (delta >= 0) & (delta <= n_steps) & (lk[:, None, :] >= 0)
    s = jnp.where(mask[None, :, None, None], s, NEG_INF)
    m = jnp.max(s, axis=-1, keepdims=True)
    e = jnp.exp(s - m)
    den = jnp.sum(e, axis=-1)
    num = jnp.einsum('bnrhqk,bnkrhe->bnqrhe', e, vc)
    num = num.reshape(B, Lp, dilation, H, E)[:, :L].reshape(B, S, H, E)

    def rows(t):
        t = t.transpose(0, 1, 4, 2, 3).reshape(B, Lp, dilation, H)
        return t[:, :L].reshape(B, S, H)

    return num, rows(den), rows(m[..., 0])


def dilated_attention(q, k, v):
    parts = [dilated_pattern(q, k, v, d, w // d) for (w, d) in DILATIONS]
    m_all = parts[0][2]
    for p in parts[1:]:
        m_all = jnp.maximum(m_all, p[2])
    num_tot = 0.0
    den_tot = 0.0
    for num, den, m in parts:
        w = jnp.exp(m - m_all)
        num_tot = num_tot + num * w[..., None]
        den_tot = den_tot + den * w
    return num_tot / den_tot[..., None]


def memory_attention(q, mk, mv):
    scale = 1.0 / math.sqrt(q.shape[-1])
    s = jnp.einsum('bqhe,bkhe->bhqk', q, mk).astype(jnp.float32) * scale
    p = jax.nn.softmax(s, axis=-1)
    return jnp.einsum('bhqk,bkhe->bqhe', p, mv.astype(jnp.float32))


def setup_inputs(seed: int = 0) -> dict:
    key = jax.random.key(seed)
    ks = jax.random.split(key, 10)
    f32 = jnp.float32
    x = jax.random.normal(ks[0], (BATCH, SEQ, D_MODEL), f32)
    mem = jax.random.normal(ks[1], (BATCH, MEM_LEN, D_MODEL), f32)
    norm_g = 1.0 + 0.02 * jax.random.normal(ks[2], (DEPTH, D_MODEL), f32)
    w_in = jax.random.normal(ks[3], (DEPTH, D_MODEL, IN_W), f32) * D_MODEL ** -0.5
    b_forget = jax.random.uniform(ks[4], (DEPTH, FOX_HEADS), f32, 1.0, 4.0)
    mem_norm_g = 1.0 + 0.02 * jax.random.normal(ks[5], (DEPTH, D_MODEL), f32)
    w_mem_kv = jax.random.normal(ks[6], (DEPTH, D_MODEL, 2 * MEM_W), f32) * D_MODEL ** -0.5
    w_out = jax.random.normal(ks[7], (DEPTH, MIX_W, D_MODEL), f32) * MIX_W ** -0.5
    final_norm_g = 1.0 + 0.02 * jax.random.normal(ks[8], (D_MODEL,), f32)
    return {'x': x, 'mem': mem, 'norm_g': norm_g, 'w_in': w_in, 'b_forget': b_forget,
            'mem_norm_g': mem_norm_g, 'w_mem_kv': w_mem_kv, 'w_out': w_out,
            'final_norm_g': final_norm_g}


def reference(x, mem, norm_g, w_in, b_forget, mem_norm_g, w_mem_kv, w_out, final_norm_g):
    B, S, _ = x.shape
    pos = jnp.arange(S, dtype=jnp.float32)
    split_idx = np.cumsum(IN_SIZES)[:-1].tolist()
    for l in range(DEPTH):
        h = rmsnorm(x, norm_g[l])
        proj = h @ w_in[l]
        (fq, fk, fv, fg, flog, dq, dk, dv, dg, mq, mg) = jnp.split(proj, split_idx, axis=-1)

        logf = jax.nn.log_sigmoid((flog + b_forget[l]).astype(jnp.float32))
        hs = (B, S, FOX_HEADS, HEAD_DIM)
        fox = forgetting_attention(fq.reshape(hs), fk.reshape(hs), fv.reshape(hs), logf)
        fox = fox.reshape(B, S, FOX_W).astype(x.dtype)

        hs = (B, S, DIL_HEADS, HEAD_DIM)
        dqr = rope_partial(dq.reshape(hs), pos)
        dkr = rope_partial(dk.reshape(hs), pos)
        dil = dilated_attention(dqr, dkr, dv.reshape(hs)).reshape(B, S, DIL_W).astype(x.dtype)

        mh = rmsnorm(mem, mem_norm_g[l])
        mk, mv = jnp.split(mh @ w_mem_kv[l], 2, axis=-1)
        ms = (B, mem.shape[1], MEM_HEADS, MEM_HEAD_DIM)
        memo = memory_attention(mq.reshape(B, S, MEM_HEADS, MEM_HEAD_DIM), mk.reshape(ms), mv.reshape(ms))
        memo = memo.reshape(B, S, MEM_W).astype(x.dtype)

        y = jnp.concatenate([fox * jax.nn.silu(fg), dil * jax.nn.silu(dg), memo * jax.nn.silu(mg)], axis=-1)
        x = x + y @ w_out[l]
    return rmsnorm(x, final_norm_g)
```

```python
import bisect
from contextlib import ExitStack

import numpy as np
import concourse.bass as bass
import concourse.mybir as mybir
from concourse.bass_utils import run_bass_kernel_spmd

F32 = mybir.dt.float32
BF16 = mybir.dt.bfloat16
AF = mybir.ActivationFunctionType
ALU = mybir.AluOpType

NCORES = 8
NB = 2
S = 2048
D = 1024
NT = 16
NCH = 8
MEM = 256
EPS = 1e-6
WSLOT = 3072
FINE = False
FILL_EVERY = 4


class Engine:
    def __init__(self, name, skip_self=False):
        self.name = name
        self.ops = []
        self.n = 0
        self.waited = {}
        self.refd = set()
        self.skip_self = skip_self
        self.sem = None
        self._sorted = None

    def resolve(self, i):
        if self._sorted is None:
            self._sorted = sorted(self.refd)
        return bisect.bisect_right(self._sorted, i)


class DSem:
    def __init__(self, name):
        self.name = name
        self.n = 0
        self.sem = None

    def resolve(self, i):
        return 16 * i


class Res:
    def __init__(self, name="", excl=False):
        self.name = name
        self.w = {}
        self.r = {}
        self.excl = excl


def emit(eng, fn, reads=(), writes=(), dsem=None):
    need = {}
    for r in reads:
        for o, i in r.w.items():
            if need.get(o, 0) < i:
                need[o] = i
        if r.excl:
            for o, i in r.r.items():
                if o is not eng and need.get(o, 0) < i:
                    need[o] = i
    for w in writes:
        for o, i in w.w.items():
            if need.get(o, 0) < i:
                need[o] = i
        for o, i in w.r.items():
            if need.get(o, 0) < i:
                need[o] = i
    for o, i in need.items():
        if o is eng and eng.skip_self:
            continue
        if eng.waited.get(o, 0) >= i:
            continue
        eng.waited[o] = i
        if isinstance(o, Engine):
            o.refd.add(i)
        eng.ops.append(("wait", o, i))
    if dsem is None:
        eng.n += 1
        tok = (eng, eng.n)
        eng.ops.append(("op", fn, eng.n))
    else:
        dsem.n += 1
        tok = (dsem, dsem.n)
        eng.ops.append(("dma", fn, dsem))
    for r in reads:
        if r.r.get(tok[0], 0) < tok[1]:
            r.r[tok[0]] = tok[1]
    for w in writes:
        w.w[tok[0]] = tok[1]
        w.r = {}
    return tok


def wait_tok(eng, obj, idx):
    if eng.waited.get(obj, 0) >= idx:
        return
    eng.waited[obj] = idx
    if isinstance(obj, Engine):
        obj.refd.add(idx)
    eng.ops.append(("wait", obj, idx))


def replay(eng, h):
    for op in eng.ops:
        if op[0] == "wait":
            h.wait_ge(op[1].sem, op[1].resolve(op[2]))
        elif op[0] == "op":
            inst = op[1](h)
            if op[2] in eng.refd:
                inst.then_inc(eng.sem, 1)
        else:
            inst = op[1](h)
            inst.then_inc(op[2].sem, 16)


def chain_gens(*gens):
    for g in gens:
        for _ in g:
            yield


def MM(out, lhsT, rhs, start=True, stop=True, sgc=False):
    return lambda t: t.matmul(out, lhsT=lhsT, rhs=rhs, start=start, stop=stop, skip_group_check=sgc)


def TP(out, in_, idn):
    return lambda t: t.transpose(out, in_, idn)


def ACTV(out, in_, func, bias=0.0, scale=1.0, accum_out=None):
    if accum_out is None:
        return lambda a: a.activation(out=out, in_=in_, func=func, bias=bias, scale=scale)
    return lambda a: a.activation(out=out, in_=in_, func=func, bias=bias, scale=scale, accum_out=accum_out)


def TT(out, in0, in1, op):
    return lambda v: v.tensor_tensor(out=out, in0=in0, in1=in1, op=op)


def TS(out, in0, s1, op0):
    return lambda v: v.tensor_scalar(out=out, in0=in0, scalar1=s1, scalar2=None, op0=op0)


def STT(out, in0, scalar, in1, op0, op1):
    return lambda v: v.scalar_tensor_tensor(out=out, in0=in0, scalar=scalar, in1=in1, op0=op0, op1=op1)


def CP(out, in_):
    return lambda v: v.tensor_copy(out=out, in_=in_)


def MS(ap, val):
    return lambda v: v.memset(ap, val)


def RC(out, in_):
    return lambda v: v.reciprocal(out=out, in_=in_)


def DMA(out, in_):
    return lambda q: q.dma_start(out=out, in_=in_)

def build_program():
    nc = bass.Bass("TRN2", target_bir_lowering=False)

    def din(name, shape):
        return nc.dram_tensor(name, list(shape), F32, kind="ExternalInput")

    x_d = din("x", [NB, S, D])
    mem_d = din("mem", [NB, MEM, D])
    wpair_d = din("wpair", [12, 128, WSLOT])
    wg8_d = din("wg8", [8, 128, 2048])
    wmq2_d = din("wmq2", [2, 128, 2048])
    wmkv4_d = din("wmkv4", [4, 128, 2048])
    wfl_d = din("wfl", [128, 96])
    wout_d = din("wout", [128, 16384])
    gf_d = din("gf", [128, 1024])
    g1T_d = din("g1T", [128, 8])
    gmT_d = din("gmT", [128, 8])
    bfor_d = din("bfor", [128, 192])
    cbf_d = din("cbf", [128, 128 * 4 + 2048])
    cf_d = din("cf", [128, 128 * 3])
    rope_d = din("rope", [128, 4096])
    out_d = nc.dram_tensor("out", [NB, S, D], F32, kind="ExternalOutput")

    PE = Engine("pe", skip_self=True)
    ACT = Engine("act")
    DVE = Engine("dve")
    POOL = Engine("pool")
    SP = Engine("sp")
    engines = [PE, ACT, DVE, POOL, SP]
    dsems = []

    def mkdsem(name):
        d = DSem(name)
        dsems.append(d)
        return d

    es = ExitStack()
    with es:
        def sb(name, free, dt):
            return es.enter_context(nc.sbuf_tensor("s_" + name, [128, free], dt))

        def ps(name, free, dt):
            return es.enter_context(nc.psum_tensor("p_" + name, [128, free], dt))

        big = sb("big", 16384, BF16)
        G = sb("G", 16 * 2048, BF16)
        Qa = [sb("Qa%d" % i, 2048, BF16) for i in range(2)]
        Qb = [sb("Qb%d" % i, 2048, BF16) for i in range(2)]
        Ka = [sb("Ka%d" % i, 2048, BF16) for i in range(2)]
        Kb = [sb("Kb%d" % i, 2048, BF16) for i in range(2)]
        VPW = 66
        Vp = [sb("Vp%d" % i, 16 * 2 * VPW, BF16) for i in range(2)]
        wsl = [sb("wsl0", WSLOT, BF16), sb("wsl1", WSLOT, BF16)]
        wfl = sb("wfl", 96, BF16)
        xt = [sb("xt0", 1024, F32), sb("xt1", 1024, F32)]
        xs = [sb("xs0", 1024, F32), sb("xs1", 1024, F32)]
        VT = xt[1].bitcast(BF16)
        tAs = [xs[0][:, 0:512], xs[0][:, 512:1024]]
        tU = xt[0][:, 512:1024]
        fl = xs[0][:, 0:192]
        Lg = xs[0][:, 192:384]
        tsb = xs[0][:, 384:576]
        pre = xs[0][:, 576:768]
        ctm = xs[0][:, 768:960]
        gf = sb("gf", 1024, F32)
        g1T = sb("g1T", 8, F32)
        gmT = sb("gmT", 8, F32)
        bfor = sb("bfor", 192, F32)
        PT = [sb("PT%d" % i, 512, BF16) for i in range(5)]
        cbf = sb("cbf", 128 * 4 + 2048, BF16)
        zer = sb("zer", 512, BF16)
        epsb = sb("epsb", 1, F32)
        oneb = sb("oneb", 1, F32)
        cf = sb("cf", 384, F32)
        ropeb = [sb("ropeb0", 1024, F32), sb("ropeb1", 1024, F32)]
        MK = sb("MK", 4 * 256, BF16)
        MVW = 130
        MVp = sb("MVp", 2 * 4 * MVW, BF16)
        tbs = [sb("tb0", 512, BF16), sb("tb1", 512, BF16)]
        yT = sb("yT", 2048, BF16)
        XW = 76
        XQ = sb("XQ", 16 * XW, BF16)
        XK = sb("XK", 16 * XW, BF16)
        P3 = sb("P3", 192 * 3, BF16)
        st = [sb("st%d" % i, 8, F32) for i in range(2)]
        rd = [sb("rd%d" % i, 4, F32) for i in range(2)]

        PJ = [ps("PJ0", 512, F32), ps("PJ1", 512, F32)]
        STp = [ps("ST%d" % i, 512, F32) for i in range(4)]
        OA = [ps("OA0", 512, F32), ps("OA1", 512, F32)]
        PJb = [t.bitcast(BF16) for t in PJ]

        R = lambda n: Res(n)
        r_hT = [R("hT%d" % i) for i in range(4)]
        r_wout = R("wout")
        r_G = [[R("G") for _ in range(16)] for _ in range(16)]
        r_Qa = [R("Qa0"), R("Qa1")]
        r_Qb = [R("Qb0"), R("Qb1")]
        r_Ka = [R("Ka0"), R("Ka1")]
        r_Kb = [R("Kb0"), R("Kb1")]
        r_Vp = [R("Vp0"), R("Vp1")]
        r_wsl = [R("wsl0"), R("wsl1")]
        r_xt = [R("xt0"), R("xt1")]
        r_xs = [R("xs0"), R("xs1")]
        r_VT = r_xt[1]
        r_sc = r_xs[0]
        r_const = R("const")
        r_zer = R("zer")
        r_PT = [R("PT") for _ in range(5)]
        r_rope = [R("rope0"), R("rope1")]
        r_MK, r_MVp = R("MK"), R("MVp")
        r_tbs = [R("tb0"), R("tb1")]
        r_tA = [R("tA0"), R("tA1")]
        r_tU = R("tU")
        r_yT = R("yT")
        r_X = R("X")
        r_P3 = R("P3")
        r_st = [R("st0"), R("st1")]
        r_rd = [R("rd0"), R("rd1")]
        r_PJ = [Res("PJ0", True), Res("PJ1", True)]
        r_ST = [Res("ST%d" % i, True) for i in range(4)]
        r_OA = [Res("OA0", True), Res("OA1", True)]

        d_x = [mkdsem("dx0"), mkdsem("dx1")]
        d_w = [mkdsem("dw0"), mkdsem("dw1")]
        d_r = [mkdsem("dr0"), mkdsem("dr1")]
        d_o = [mkdsem("do0"), mkdsem("do1")]
        d_wo = mkdsem("dwo")

        def AP(t, off, dims):
            return bass.AP(t, off, [list(d) for d in dims])

        ident = cbf[:, 0:128]
        negmask = cbf[:, 128:256]
        pswap = cbf[:, 256:384]
        WM0 = 384
        farmask = cbf[:, 384 + 2048:384 + 2048 + 128]
        identf = cf[:, 0:128]
        negtri = cf[:, 128:256]
        negones = cf[:, 256:384]

        def load_const(q, dst_ap, src_ap, name):
            d = mkdsem("dc_" + name)
            emit(q, DMA(dst_ap, src_ap), dsem=d)
            r_const.w[d] = 1

        load_const(POOL, cbf[:], cbf_d.ap(), "cbf")
        load_const(POOL, wfl[:], wfl_d.ap(), "wfl")
        load_const(SP, cf[:], cf_d.ap(), "cf")
        load_const(SP, gf[:], gf_d.ap(), "gf")
        load_const(SP, g1T[:], g1T_d.ap(), "g1T")
        load_const(SP, gmT[:], gmT_d.ap(), "gmT")
        load_const(SP, bfor[:], bfor_d.ap(), "bfor")
        emit(DVE, MS(zer[:], 0.0), writes=[r_zer])
        emit(DVE, MS(epsb[:], EPS), writes=[r_zer])
        emit(DVE, MS(oneb[:], 1.0), writes=[r_zer])
        for i in range(2):
            emit(DVE, MS(Qa[i][:], 0.0), writes=[r_Qa[i]])
            emit(DVE, MS(Qb[i][:], 0.0), writes=[r_Qb[i]])
            emit(DVE, MS(Ka[i][:], 0.0), writes=[r_Ka[i]])
            emit(DVE, MS(Kb[i][:], 0.0), writes=[r_Kb[i]])
            emit(DVE, MS(AP(Vp[i], 64, [[16 * 2 * VPW, 128], [VPW, 32], [1, 1]]), 1.0), writes=[r_Vp[i]])
        emit(DVE, MS(AP(MVp, 128, [[2 * 4 * MVW, 128], [MVW, 8], [1, 1]]), 1.0), writes=[r_MVp])
        emit(DVE, MS(XQ[:], 0.0), writes=[r_X])
        emit(DVE, MS(XK[:], 0.0), writes=[r_X])
        for off in (3, 73):
            emit(DVE, MS(AP(XQ, off, [[16 * XW, 128], [XW, 16], [1, 3]]), 1.0), writes=[r_X])
        for off in (0, 70):
            emit(DVE, MS(AP(XK, off, [[16 * XW, 128], [XW, 16], [1, 3]]), 1.0), writes=[r_X])

        cnt = dict(pj=0, x=0, w=0, st=0, pt=0, oa=0, rope=0)

        def nxt(key, mod):
            k = cnt[key] % mod
            cnt[key] += 1
            return k

        def load_weights(src_ap, ncols_total):
            k = nxt("w", 2)
            emit(POOL, DMA(wsl[k][:, 0:ncols_total], src_ap), writes=[r_wsl[k]], dsem=d_w[k])
            return k

        def rms_stats(src_ap, k, r_src):
            emit(ACT, ACTV(xs[k][:], src_ap, AF.Square, accum_out=st[k][:, 0:1]),
                 reads=[r_src], writes=[r_xs[k], r_st[k]] + (r_tA if k == 0 else []))
            emit(ACT, ACTV(st[k][:, 1:2], st[k][:, 0:1], AF.Ln, bias=epsb[:, 0:1], scale=1.0 / D),
                 reads=[r_st[k], r_zer], writes=[r_st[k]])
            emit(ACT, ACTV(st[k][:, 2:3], st[k][:, 1:2], AF.Exp, scale=-0.5),
                 reads=[r_st[k]], writes=[r_st[k]])

        def norm_part1(src_rows):
            k = nxt("x", 2)
            emit(SP, DMA(xt[k][:], src_rows), writes=[r_xt[k]] + ([r_tU] if k == 0 else []), dsem=d_x[k])
            rms_stats(xt[k][:], k, r_xt[k])
            emit(DVE, TS(xs[k][:], xt[k][:], st[k][:, 2:3], ALU.mult),
                 reads=[r_xt[k], r_st[k]], writes=[r_xs[k]])
            return k

        def norm_part2(k, gT, dst_t, dst_free, dst_stride, dst_off, r_dst):
            for half in range(2):
                tk = nxt("pj", 2)
                for q in range(4):
                    c = half * 4 + q
                    emit(PE, TP(PJ[tk][:, q * 128:(q + 1) * 128], xs[k][:, c * 128:(c + 1) * 128], identf),
                         reads=[r_xs[k], r_const], writes=[r_PJ[tk]])
                emit(DVE, TT(AP(dst_t, half * 4 * dst_stride + dst_off, [[dst_free, 128], [dst_stride, 4], [1, 128]]),
                             AP(PJ[tk], 0, [[512, 128], [128, 4], [1, 128]]),
                             AP(gT, half * 4, [[8, 128], [1, 4], [0, 128]]), ALU.mult),
                     reads=[r_PJ[tk], r_const], writes=[r_dst])

        def attention(kind, qT, kT, kbase, v_of, E, nblk, gcol, grp, r_q, r_k, r_v, filler=None, fill_every=FILL_EVERY):
            W = 512 // nblk
            nchunk = NT // nblk
            tiles = []
            for qc in range(nchunk):
                i0 = qc * nblk
                js = [0, 1] if kind == "mem" else list(range(i0 + nblk))
                for j in js:
                    ifirst = i0 if kind == "mem" else max(i0, j)
                    tiles.append(dict(qc=qc, j=j, ifirst=ifirst, ilast=i0 + nblk - 1, i0=i0,
                                      first=(j == js[0]), last=(j == js[-1])))
            state = {}

            def stage_A(tl):
                s = nxt("st", 4)
                tl["s"] = s
                if tl["first"]:
                    o = nxt("oa", 2)
                    state["o"] = o
                    emit(PE, MM(OA[o][:], zer[:, 0:128], zer[:, 0:512], start=True, stop=False, sgc=True),
                         reads=[r_zer], writes=[r_OA[o]])
                tl["o"] = state["o"]
                j = tl["j"]
                n0 = tl["ifirst"] * 128
                N = (tl["ilast"] - tl["ifirst"] + 1) * 128
                tl["N"] = N
                diag = (kind == "fox") and (tl["ifirst"] == j)
                far = (kind == "dil") and (tl["ifirst"] - j >= 5)
                tl["far"] = far
                emit(PE, MM(STp[s][:, 0:N], kT[:, kbase + j * 128:kbase + (j + 1) * 128], qT[:, n0:n0 + N],
                            start=True, stop=not (diag or far), sgc=True),
                     reads=[r_q, r_k], writes=[r_ST[s]])
                if far:
                    nb_ = N // 128
                    for bi in range(nb_):
                        emit(PE, MM(STp[s][:, bi * 128:(bi + 1) * 128], ident, farmask, start=False, stop=(bi == nb_ - 1), sgc=True),
                             reads=[r_const], writes=[r_ST[s]])
                if diag:
                    emit(PE, MM(STp[s][:, 0:128], ident, negmask, start=False, stop=True, sgc=True),
                         reads=[r_const], writes=[r_ST[s]])

            def stage_B(tl):
                s = tl["s"]
                p = nxt("pt", 5)
                tl["p"] = p
                N = tl["N"]
                j = tl["j"]
                if kind == "fox":
                    emit(ACT, ACTV(PT[p][:, 0:N], STp[s][:, 0:N], AF.Exp, scale=0.125),
                         reads=[r_ST[s]], writes=[r_PT[p]])
                elif kind == "dil":
                    emit(ACT, ACTV(PT[p][:, 0:N], STp[s][:, 0:N], AF.Exp, scale=0.125),
                         reads=[r_ST[s]], writes=[r_PT[p]])
                    d0 = tl["ifirst"] - j
                    if not tl["far"]:
                        emit(DVE, TT(PT[p][:, 0:N], PT[p][:, 0:N], cbf[:, WM0 + d0 * 128:WM0 + d0 * 128 + N], ALU.mult),
                             reads=[r_PT[p], r_const], writes=[r_PT[p]])
                else:
                    emit(ACT, ACTV(PT[p][:, 0:N], STp[s][:, 0:N], AF.Exp, scale=float(128.0 ** -0.5)),
                         reads=[r_ST[s]], writes=[r_PT[p]])

            def stage_C(tl):
                p = tl["p"]
                o = tl["o"]
                j = tl["j"]
                nb_ = tl["ilast"] - tl["ifirst"] + 1
                for bi, i in enumerate(range(tl["ifirst"], tl["ilast"] + 1)):
                    blk = i - tl["i0"]
                    lastmm = tl["last"] and (bi == nb_ - 1)
                    emit(PE, MM(OA[o][:, blk * W:blk * W + E + 1], PT[p][:, bi * 128:(bi + 1) * 128], v_of(j),
                                start=False, stop=lastmm, sgc=True),
                         reads=[r_PT[p], r_v], writes=[r_OA[o]])
                if tl["last"]:
                    emit(DVE, RC(rd[o][:, 0:nblk], AP(OA[o], E, [[512, 128], [W, nblk]])),
                         reads=[r_OA[o]], writes=[r_rd[o]])
                    for blk in range(nblk):
                        tt = tl["i0"] + blk
                        ga = G[:, tt * 2048 + gcol:tt * 2048 + gcol + E]
                        emit(DVE, STT(ga, OA[o][:, blk * W:blk * W + E], rd[o][:, blk:blk + 1], ga, ALU.mult, ALU.mult),
                             reads=[r_OA[o], r_rd[o], r_G[tt][grp]], writes=[r_G[tt][grp]])

            LA = 2 if kind == "mem" else 4
            for n in range(len(tiles) + LA):
                if n >= LA:
                    stage_C(tiles[n - LA])
                if filler is not None and n % fill_every == fill_every - 1:
                    next(filler, None)
                if n < len(tiles):
                    stage_A(tiles[n])
                    stage_B(tiles[n])

        def hT_ap(c, t0, n):
            return big[:, c * 2048 + t0:c * 2048 + t0 + n]

        def proj_pair(pp, slot):
            is_dil = pp >= 6
            wk = load_weights(wpair_d.ap()[pp], WSLOT)
            QA, QB, KA, KB, VP = Qa[slot], Qb[slot], Ka[slot], Kb[slot], Vp[slot]
            rQA, rQB, rKA, rKB, rVP = r_Qa[slot], r_Qb[slot], r_Ka[slot], r_Kb[slot], r_Vp[slot]
            if pp in (6, 7):
                emit(DVE, MS(QA[64:70, :], 0.0), writes=[rQA])
                emit(DVE, MS(KA[64:70, :], 0.0), writes=[rKA])
                emit(DVE, MS(QB[0:6, :], 0.0), writes=[rQB])
                emit(DVE, MS(KB[0:6, :], 0.0), writes=[rKB])
            units = []
            ropek = {}

            def mk_unit(tc, kind, u):
                cs = slice(tc * 512, (tc + 1) * 512)
                st_ = {}

                def P1():
                    if is_dil and kind == 0:
                        rk = nxt("rope", 2)
                        ropek[tc] = rk
                        emit(SP, DMA(AP(ropeb[rk], 0, [[1024, 128], [512, 2], [1, 512]]),
                                     AP(rope_d, tc * 512, [[4096, 128], [2048, 2], [1, 512]])),
                             writes=[r_rope[rk]], dsem=d_r[rk])
                    pk = nxt("pj", 2)
                    st_["pk"] = pk
                    for c in range(NCH):
                        emit(PE, MM(PJ[pk][:], wsl[wk][:, c * 384 + kind * 128:c * 384 + kind * 128 + 128],
                                    hT_ap(c, tc * 512, 512), start=(c == 0), stop=(c == NCH - 1)),
                             reads=[r_wsl[wk], r_hT[tc]], writes=[r_PJ[pk]])
                        if FINE and c % 2 == 1 and c < NCH - 1:
                            yield

                def P2():
                    pk = st_["pk"]
                    if kind == 2:
                        emit(ACT, ACTV(VT[:, cs], PJ[pk][:], AF.Copy), reads=[r_PJ[pk]], writes=[r_VT])
                    elif not is_dil:
                        if kind == 0:
                            emit(DVE, CP(QA[0:64, cs], PJ[pk][0:64, :]), reads=[r_PJ[pk]], writes=[rQA])
                            emit(DVE, CP(QB[64:128, cs], PJ[pk][64:128, :]), reads=[r_PJ[pk]], writes=[rQB])
                        else:
                            emit(DVE, CP(KA[0:64, cs], PJ[pk][0:64, :]), reads=[r_PJ[pk]], writes=[rKA])
                            emit(DVE, CP(KB[64:128, cs], PJ[pk][64:128, :]), reads=[r_PJ[pk]], writes=[rKB])
                    else:
                        a = u % 2
                        rk = ropek[tc]
                        emit(ACT, ACTV(tbs[a][:], PJ[pk][:], AF.Copy), reads=[r_PJ[pk]], writes=[r_tbs[a]])
                        emit(DVE, TT(tAs[a], PJ[pk][:], ropeb[rk][:, 0:512], ALU.mult),
                             reads=[r_PJ[pk], r_rope[rk]], writes=[r_tA[a], r_xs[0]])

                def P3():
                    if not is_dil or kind == 2:
                        return
                    a = u % 2
                    rk = ropek[tc]
                    tk = nxt("pj", 2)
                    emit(PE, MM(PJ[tk][:], pswap, tbs[a][:]), reads=[r_tbs[a], r_const], writes=[r_PJ[tk]])
                    emit(DVE, TT(tU, PJ[tk][:], ropeb[rk][:, 512:1024], ALU.mult),
                         reads=[r_PJ[tk], r_rope[rk]], writes=[r_tU, r_xt[0]])
                    tA0, tA1 = xs[0][0:64, a * 512:(a + 1) * 512], xs[0][64:128, a * 512:(a + 1) * 512]
                    tU0, tU1 = xt[0][0:64, 512:1024], xt[0][64:128, 512:1024]
                    if kind == 0:
                        emit(POOL, TT(QA[0:64, cs], tA0, tU0, ALU.add), reads=[r_tA[a], r_tU], writes=[rQA])
                        emit(POOL, TT(QB[64:128, cs], tA1, tU1, ALU.add), reads=[r_tA[a], r_tU], writes=[rQB])
                    else:
                        emit(POOL, TT(KA[0:64, cs], tA0, tU0, ALU.add), reads=[r_tA[a], r_tU], writes=[rKA])
                        emit(POOL, TT(KB[64:128, cs], tA1, tU1, ALU.add), reads=[r_tA[a], r_tU], writes=[rKB])

                return (P1, P2, P3)

            u = 0
            for tc in range(4):
                for kind in range(3):
                    units.append(mk_unit(tc, kind, u))
                    u += 1

            def mk_vt(h8):
                def P1():
                    tk = nxt("pj", 2)
                    for q in range(8):
                        tt = h8 * 8 + q
                        emit(PE, TP(PJb[tk][:, q * 128:(q + 1) * 128], VT[:, tt * 128:(tt + 1) * 128], ident),
                             reads=[r_VT, r_const], writes=[r_PJ[tk]])
                    emit(DVE, CP(AP(VP, h8 * 8 * 2 * VPW, [[16 * 2 * VPW, 128], [2 * VPW, 8], [VPW, 2], [1, 64]]),
                                 AP(PJb[tk], 0, [[1024, 128], [128, 8], [64, 2], [1, 64]])),
                         reads=[r_PJ[tk]], writes=[rVP])
                return (P1, None, None)

            def mk_aug(hl, which, h8):
                hidx = pp * 2 + hl
                qo, ko = (70, 73) if hl == 0 else (0, 3)
                c0, ncol, p0 = (6, 70, 64) if hl == 0 else (0, 6, 0)
                if which == 0:
                    X, dst, rdst = XQ, (QA if hl == 0 else QB), (rQA if hl == 0 else rQB)
                else:
                    X, dst, rdst = XK, (KA if hl == 0 else KB), (rKA if hl == 0 else rKB)

                def P1():
                    if h8 == 0:
                        if which == 0:
                            emit(DVE, CP(AP(XQ, qo, [[16 * XW, 128], [XW, 16], [1, 3]]),
                                         AP(P3, hidx * 3, [[576, 128], [36, 16], [1, 3]])),
                                 reads=[r_P3], writes=[r_X])
                        else:
                            emit(DVE, TS(AP(XK, ko, [[16 * XW, 128], [XW, 16], [1, 3]]),
                                         AP(P3, hidx * 3, [[576, 128], [36, 16], [1, 3]]), -1.0, ALU.mult),
                                 reads=[r_P3], writes=[r_X])
                    tk = nxt("pj", 2)
                    for q in range(8):
                        tt = h8 * 8 + q
                        emit(PE, TP(PJb[tk][0:ncol, q * 128:(q + 1) * 128],
                                    X[:, tt * XW + c0:tt * XW + c0 + ncol], ident),
                             reads=[r_X, r_const], writes=[r_PJ[tk]])
                    emit(DVE, CP(dst[p0:p0 + 6, h8 * 1024:(h8 + 1) * 1024], PJb[tk][p0:p0 + 6, 0:1024]),
                         reads=[r_PJ[tk]], writes=[rdst])
                return (P1, None, None)

            tail = [mk_vt(0), mk_vt(1)]
            if not is_dil:
                for hl in range(2):
                    for which in range(2):
                        for h8 in range(2):
                            tail.append(mk_aug(hl, which, h8))
            nu = len(units)
            for k in range(nu + 2):
                if 0 <= k - 2 < nu:
                    units[k - 2][2]()
                if 0 <= k - 1 < nu:
                    units[k - 1][1]()
                if k < nu:
                    for _ in units[k][0]():
                        yield
                yield
            for t_ in tail:
                t_[0]()
                yield

        for b in range(NB):
            mhT = yT
            jobs = [(x_d.ap()[b, tt * 128:(tt + 1) * 128, :], (g1T, big, 16384, 2048, tt * 128, r_hT[tt // 4])) for tt in range(NT)]
            jobs += [(mem_d.ap()[b, mt * 128:(mt + 1) * 128, :], (gmT, mhT, 2048, 256, mt * 128, r_yT)) for mt in range(2)]
            kprev = norm_part1(jobs[0][0])
            for ji in range(len(jobs)):
                knext = norm_part1(jobs[ji + 1][0]) if ji + 1 < len(jobs) else None
                norm_part2(kprev, *jobs[ji][1])
                kprev = knext
            def stage_c1():
                pk = nxt("pj", 2)
                for tt in range(NT):
                    for c in range(NCH):
                        emit(PE, MM(PJ[pk][:, tt * 12:(tt + 1) * 12], hT_ap(c, tt * 128, 128), wfl[:, c * 12:(c + 1) * 12],
                                    start=(c == 0), stop=(c == NCH - 1)),
                             reads=[r_const, r_hT[tt // 4]], writes=[r_PJ[pk]])
                emit(DVE, TT(fl, PJ[pk][:, 0:192], bfor[:], ALU.add), reads=[r_PJ[pk], r_const], writes=[r_sc] + r_tA)
                emit(ACT, ACTV(Lg, fl, AF.Exp, scale=-1.0), reads=[r_sc], writes=[r_sc])
                emit(ACT, ACTV(Lg, Lg, AF.Ln, bias=oneb[:, 0:1], scale=1.0), reads=[r_sc, r_zer], writes=[r_sc])

            def stage_c2():
                pk1 = nxt("pj", 2)
                emit(PE, MM(PJ[pk1][:, 0:192], negtri, Lg), reads=[r_sc, r_const], writes=[r_PJ[pk1]])
                pk2 = nxt("pj", 2)
                emit(PE, MM(PJ[pk2][:, 0:192], negones, Lg), reads=[r_sc, r_const], writes=[r_PJ[pk2]])
                emit(DVE, CP(tsb, PJ[pk2][:, 0:192]), reads=[r_PJ[pk2]], writes=[r_sc])
                emit(DVE, MS(pre[:, 0:12], 0.0), writes=[r_sc])
                for tt in range(1, NT):
                    emit(DVE, TT(pre[:, tt * 12:(tt + 1) * 12], pre[:, (tt - 1) * 12:tt * 12], tsb[:, (tt - 1) * 12:tt * 12], ALU.add),
                         reads=[r_sc], writes=[r_sc])
                emit(DVE, TT(ctm, PJ[pk1][:, 0:192], pre, ALU.add), reads=[r_PJ[pk1], r_sc], writes=[r_sc])

            def stage_c3():
                p3 = lambda t, k: AP(t, k, [[576, 128], [3, 192]])
                emit(DVE, TS(p3(P3, 0), ctm, 8.0, ALU.mult), reads=[r_sc], writes=[r_P3])
                emit(DVE, STT(fl, ctm, 8.0, p3(P3, 0), ALU.mult, ALU.subtract), reads=[r_sc, r_P3], writes=[r_sc])
                emit(DVE, CP(p3(P3, 1), fl), reads=[r_sc], writes=[r_P3])
                emit(DVE, TT(Lg, fl, p3(P3, 1), ALU.subtract), reads=[r_sc, r_P3], writes=[r_sc])
                emit(DVE, CP(p3(P3, 2), Lg), reads=[r_sc], writes=[r_P3])


            stage_c1()
            gen0 = None
            for g8 in range(8):
                wk = load_weights(wg8_d.ap()[g8], 2048)
                for t2 in range(NT // 2):
                    pk = nxt("pj", 2)
                    for hf in range(2):
                        tt = 2 * t2 + hf
                        for c in range(NCH):
                            emit(PE, MM(PJ[pk][:, hf * 256:(hf + 1) * 256], hT_ap(c, tt * 128, 128),
                                        wsl[wk][:, c * 256:(c + 1) * 256], start=(c == 0), stop=(c == NCH - 1)),
                                 reads=[r_wsl[wk], r_hT[tt // 4]], writes=[r_PJ[pk]])
                    emit(ACT, ACTV(AP(G, 2 * t2 * 2048 + g8 * 256, [[16 * 2048, 128], [2048, 2], [1, 256]]),
                                   AP(PJ[pk], 0, [[512, 128], [256, 2], [1, 256]]), AF.Silu),
                         reads=[r_PJ[pk]],
                         writes=[r_G[2 * t2][2 * g8], r_G[2 * t2][2 * g8 + 1], r_G[2 * t2 + 1][2 * g8], r_G[2 * t2 + 1][2 * g8 + 1]])
                    if gen0 is not None:
                        for _ in range(4 if FINE else 1):
                            next(gen0, None)
                if g8 == 0:
                    stage_c2()
                elif g8 == 1:
                    stage_c3()
                elif g8 == 5:
                    gen0 = proj_pair(0, 0)

            for _ in gen0:
                pass
            MQ = [Qa[0], Qb[0], Ka[0], Kb[0]]
            r_MQ = [r_Qa[0], r_Qb[0], r_Ka[0], r_Kb[0]]

            def proj_memkv():
                for k4 in range(4):
                    wk = load_weights(wmkv4_d.ap()[k4], 2048)
                    if k4 < 2:
                        for hl in range(2):
                            hh = 2 * k4 + hl
                            pk = nxt("pj", 2)
                            for c in range(NCH):
                                emit(PE, MM(PJ[pk][:, 0:256], wsl[wk][:, c * 256 + hl * 128:c * 256 + hl * 128 + 128],
                                            mhT[:, c * 256:(c + 1) * 256], start=(c == 0), stop=(c == NCH - 1)),
                                     reads=[r_wsl[wk], r_yT], writes=[r_PJ[pk]])
                            emit(DVE, CP(MK[:, hh * 256:(hh + 1) * 256], PJ[pk][:, 0:256]),
                                 reads=[r_PJ[pk]], writes=[r_MK])
                            yield
                    else:
                        h0 = 2 * (k4 - 2)
                        for mt in range(2):
                            pk = nxt("pj", 2)
                            for c in range(NCH):
                                emit(PE, MM(PJ[pk][:, 0:256], mhT[:, c * 256 + mt * 128:c * 256 + mt * 128 + 128],
                                            wsl[wk][:, c * 256:(c + 1) * 256], start=(c == 0), stop=(c == NCH - 1)),
                                     reads=[r_wsl[wk], r_yT], writes=[r_PJ[pk]])
                            emit(DVE, CP(AP(MVp, (mt * 4 + h0) * MVW, [[2 * 4 * MVW, 128], [MVW, 2], [1, 128]]),
                                         AP(PJ[pk], 0, [[512, 128], [128, 2], [1, 128]])),
                                 reads=[r_PJ[pk]], writes=[r_MVp])
                            yield

            def proj_mem():
                pend = None
                for q2 in range(2):
                    wk = load_weights(wmq2_d.ap()[q2], 2048)
                    for hl in range(2):
                        hh = q2 * 2 + hl
                        for tc in range(4):
                            pk = nxt("pj", 2)
                            for c in range(NCH):
                                emit(PE, MM(PJ[pk][:], wsl[wk][:, c * 256 + hl * 128:c * 256 + hl * 128 + 128],
                                            hT_ap(c, tc * 512, 512), start=(c == 0), stop=(c == NCH - 1)),
                                     reads=[r_wsl[wk], r_hT[tc]], writes=[r_PJ[pk]])
                                if FINE and c % 2 == 1 and c < NCH - 1:
                                    yield
                            if pend is not None:
                                pend()
                            if tc % 2 == 0:
                                pend = (lambda hh, tc, pk: lambda: emit(
                                    DVE, CP(MQ[hh][:, tc * 512:(tc + 1) * 512], PJ[pk][:]), reads=[r_PJ[pk]], writes=[r_MQ[hh]]))(hh, tc, pk)
                            else:
                                pend = (lambda hh, tc, pk: lambda: emit(
                                    ACT, ACTV(MQ[hh][:, tc * 512:(tc + 1) * 512], PJ[pk][:], AF.Copy), reads=[r_PJ[pk]], writes=[r_MQ[hh]]))(hh, tc, pk)
                            yield
                pend()
                yield

            for pp in range(12):
                slot = pp % 2
                is_dil = pp >= 6
                nxt_gen = proj_pair(pp + 1, (pp + 1) % 2) if pp + 1 < 12 else chain_gens(proj_memkv(), proj_mem())
                for hl in range(2):
                    hidx = (pp % 6) * 2 + hl
                    qT = Qa[slot] if hl == 0 else Qb[slot]
                    kT = Ka[slot] if hl == 0 else Kb[slot]
                    r_q = r_Qa[slot] if hl == 0 else r_Qb[slot]
                    r_k = r_Ka[slot] if hl == 0 else r_Kb[slot]
                    v_of = (lambda hl, VP: lambda j: VP[:, (j * 2 + hl) * VPW:(j * 2 + hl) * VPW + 65])(hl, Vp[slot])
                    if not is_dil:
                        attention("fox", qT, kT, 0, v_of, 64, 4, hidx * 64, pp, r_q, r_k, r_Vp[slot], filler=nxt_gen)
                    else:
                        attention("dil", qT, kT, 0, v_of, 64, 4, 768 + hidx * 64, pp, r_q, r_k, r_Vp[slot], filler=nxt_gen)
                if nxt_gen is not None:
                    for _ in nxt_gen:
                        pass

            for q4 in range(4):
                emit(POOL, DMA(big[:, q4 * 4096:(q4 + 1) * 4096], wout_d.ap()[:, q4 * 4096:(q4 + 1) * 4096]),
                     writes=([r_wout] + r_hT) if q4 == 0 else [], dsem=d_wo)
            for r_ in [r_wout] + r_hT:
                r_.w[d_wo] = d_wo.n
            for hh in range(4):
                v_of = (lambda hh: lambda j: MVp[:, (j * 4 + hh) * MVW:(j * 4 + hh) * MVW + 129])(hh)
                attention("mem", MQ[hh], MK, hh * 256, v_of, 128, 2, 1536 + hh * 128, 12 + hh, r_MQ[hh], r_MK, r_MVp)
            if b + 1 < NB:
                for hh in range(4):
                    emit(POOL, MS(MQ[hh][:], 0.0), writes=[r_MQ[hh]])

            yTb = [yT, Qa[1]]
            r_yTb = [r_yT, r_Qa[1]]

            FT = [STp[0].bitcast(BF16), STp[1].bitcast(BF16), OA[0].bitcast(BF16), OA[1].bitcast(BF16)]
            r_FT = [r_ST[0], r_ST[1], r_OA[0], r_OA[1]]
            FM = [PJ[0], PJ[1], STp[2], STp[3]]
            r_FM = [r_PJ[0], r_PJ[1], r_ST[2], r_ST[3]]
            fcnt = dict(t=0, m=0)

            def f_transposes(tt):
                yk = tt % 2
                for h8 in range(2):
                    tk = fcnt["t"] % 4
                    fcnt["t"] += 1
                    for q in range(8):
                        c = h8 * 8 + q
                        emit(PE, TP(FT[tk][:, q * 128:(q + 1) * 128], G[:, tt * 2048 + c * 128:tt * 2048 + (c + 1) * 128], ident),
                             reads=[r_G[tt][c], r_const], writes=[r_FT[tk]])
                    emit(DVE, CP(yTb[yk][:, h8 * 1024:(h8 + 1) * 1024], FT[tk][:, 0:1024]),
                         reads=[r_FT[tk]], writes=[r_yTb[yk]])

            f_transposes(0)
            for tt in range(NT):
                yk = tt % 2
                k = nxt("x", 2)
                emit(SP, DMA(xt[k][:], x_d.ap()[b, tt * 128:(tt + 1) * 128, :]), writes=[r_xt[k]] + ([r_tU] if k == 0 else []), dsem=d_x[k])
                if tt + 1 < NT:
                    f_transposes(tt + 1)
                pks = []
                for half in range(2):
                    pk = fcnt["m"] % 4
                    fcnt["m"] += 1
                    pks.append(pk)
                    for c in range(16):
                        emit(PE, MM(FM[pk][:], yTb[yk][:, c * 128:(c + 1) * 128],
                                    big[:, c * 1024 + half * 512:c * 1024 + (half + 1) * 512],
                                    start=(c == 0), stop=(c == 15)),
                             reads=[r_yTb[yk], r_wout] + r_hT, writes=[r_FM[pk]])
                for half in range(2):
                    pk = pks[half]
                    emit(DVE, TT(xt[k][:, half * 512:(half + 1) * 512], FM[pk][:], xt[k][:, half * 512:(half + 1) * 512], ALU.add),
                         reads=[r_FM[pk], r_xt[k]], writes=[r_xt[k]])
                rms_stats(xt[k][:], k, r_xt[k])
                emit(DVE, STT(xs[k][:], xt[k][:], st[k][:, 2:3], gf[:], ALU.mult, ALU.mult),
                     reads=[r_xt[k], r_st[k], r_const], writes=[r_xs[k]])
                emit(SP, DMA(out_d.ap()[b, tt * 128:(tt + 1) * 128, :], xs[k][:]), reads=[r_xs[k]], dsem=d_o[k])
            if b + 1 < NB:
                emit(POOL, MS(Qa[1][:], 0.0), writes=[r_Qa[1]])

        for k in range(2):
            wait_tok(SP, d_o[k], d_o[k].n)

        for e in engines:
            e.sem = es.enter_context(nc.semaphore("sem_" + e.name))
        for d in dsems:
            d.sem = es.enter_context(nc.semaphore("dsem_" + d.name))
        with nc.Block() as block:
            @block.tensor
            def _(t):
                replay(PE, t)

            @block.scalar
            def _(a):
                replay(ACT, a)

            @block.vector
            def _(v):
                replay(DVE, v)

            @block.gpsimd
            def _(g):
                replay(POOL, g)

            @block.sync
            def _(s):
                replay(SP, s)
    return nc


def _chunked(w, ncols):
    return np.ascontiguousarray(w.reshape(8, 128, ncols).transpose(1, 0, 2).reshape(128, 8 * ncols))


def _constants():
    p = np.arange(128)
    ident = np.eye(128, dtype=np.float32)
    negmask = np.where(p[:, None] <= p[None, :], 0.0, -30000.0).astype(np.float32)
    pswap = np.zeros((128, 128), np.float32)
    for m in range(128):
        ml = m % 64
        if ml < 8:
            pswap[m + 8, m] = -1.0
        elif ml < 16:
            pswap[m - 8, m] = 1.0
    wmask = np.zeros((128, 16, 128), np.float32)
    for dlt in range(16):
        delta = 128 * dlt + p[None, :] - p[:, None]
        m1 = (delta >= 0) & (delta <= 128)
        m2 = (delta >= 0) & (delta % 4 == 0) & (delta <= 512)
        m3 = (delta >= 0) & (delta % 16 == 0) & (delta <= 2048)
        wmask[:, dlt, :] = m1.astype(np.float32) + m2 + m3
    farmask = np.where((p[None, :] - p[:, None]) % 16 == 0, 0.0, -30000.0).astype(np.float32)
    cbf = np.concatenate([ident, negmask, pswap, wmask.reshape(128, 2048), farmask], axis=1)
    negtri = -(p[:, None] <= p[None, :]).astype(np.float32)
    negones = -np.ones((128, 128), np.float32)
    cf = np.concatenate([ident, negtri, negones], axis=1)
    pos = np.arange(S, dtype=np.float32)
    inv_freq = (1.0 / (np.float32(500000.0) ** (np.arange(0, 16, 2, dtype=np.float32) / np.float32(16)))).astype(np.float32)
    ang = (pos[:, None] * inv_freq[None, :]).astype(np.float32)
    C = np.ones((128, S), np.float32)
    Sn = np.zeros((128, S), np.float32)
    for m in range(128):
        ml = m % 64
        if ml < 16:
            C[m] = np.cos(ang[:, ml % 8])
            Sn[m] = np.sin(ang[:, ml % 8])
    rope = np.concatenate([C, Sn], axis=1).astype(np.float32)
    return np.ascontiguousarray(cbf), np.ascontiguousarray(cf), np.ascontiguousarray(rope)


_PROGRAM = None


def kernel(x, mem, norm_g, w_in, b_forget, mem_norm_g, w_mem_kv, w_out, final_norm_g):
    global _PROGRAM
    x = np.asarray(x, dtype=np.float32)
    mem = np.asarray(mem, dtype=np.float32)
    w = np.asarray(w_in, dtype=np.float32)[0]
    sizes = [768] * 4 + [12] + [768] * 4 + [512] * 2
    offs = np.cumsum([0] + sizes)
    fq, fk, fv, fg, flg, dq, dk, dv, dg, mq, mg = [w[:, offs[i]:offs[i + 1]] for i in range(11)]
    wpair = np.zeros((12, 128, WSLOT), np.float32)
    for pp in range(6):
        cs = slice(pp * 128, (pp + 1) * 128)
        wpair[pp] = _chunked(np.concatenate([fq[:, cs], fk[:, cs], fv[:, cs]], axis=1), 384)
        wpair[6 + pp] = _chunked(np.concatenate([dq[:, cs], dk[:, cs], dv[:, cs]], axis=1), 384)
    gates = np.concatenate([fg, dg, mg], axis=1)
    wg8 = np.stack([_chunked(gates[:, i * 256:(i + 1) * 256], 256) for i in range(8)])
    wmq2 = np.stack([_chunked(mq[:, i * 256:(i + 1) * 256], 256) for i in range(2)])
    wkv = np.asarray(w_mem_kv, dtype=np.float32)[0]
    wmkv4 = np.stack([_chunked(wkv[:, i * 256:(i + 1) * 256], 256) for i in range(4)])
    wfl = _chunked(flg, 12)
    wo = np.asarray(w_out, dtype=np.float32)[0]
    wout = np.ascontiguousarray(wo.reshape(16, 128, 1024).transpose(1, 0, 2).reshape(128, 16384))
    gfb = np.ascontiguousarray(np.broadcast_to(np.asarray(final_norm_g, np.float32)[None, :], (128, 1024)))
    g1T = np.ascontiguousarray(np.asarray(norm_g, np.float32)[0].reshape(8, 128).T)
    gmT = np.ascontiguousarray(np.asarray(mem_norm_g, np.float32)[0].reshape(8, 128).T)
    bfor = np.ascontiguousarray(np.broadcast_to(np.asarray(b_forget, np.float32)[0][None, None, :], (128, 16, 12)).reshape(128, 192))
    cbf, cf, rope = _constants()

    if _PROGRAM is None:
        _PROGRAM = build_program()
    nc = _PROGRAM
    shared = dict(wpair=wpair, wg8=wg8, wmq2=wmq2, wmkv4=wmkv4, wfl=wfl, wout=wout, gf=gfb, g1T=g1T, gmT=gmT,
                  bfor=bfor, cbf=cbf, cf=cf, rope=rope)
    in_maps = []
    for c in range(NCORES):
        m = dict(shared)
        m["x"] = np.ascontiguousarray(x[c * NB:(c + 1) * NB])
        m["mem"] = np.ascontiguousarray(mem[c * NB:(c + 1) * NB])
        in_maps.append(m)
    res = run_bass_kernel_spmd(nc, in_maps, core_ids=list(range(NCORES)))
    out = np.concatenate([np.asarray(r["out"], dtype=np.float32) for r in res.results], axis=0)
    return out
```

```python
import bisect
from contextlib import ExitStack

import numpy as np
import concourse.bass as bass
import concourse.mybir as mybir
from concourse.bass_utils import run_bass_kernel_spmd

F32 = mybir.dt.float32
BF16 = mybir.dt.bfloat16
AF = mybir.ActivationFunctionType
ALU = mybir.AluOpType

NCORES = 8
NB = 2
S = 2048
D = 1024
NT = 16
NCH = 8
MEM = 256
EPS = 1e-6
WSLOT = 3072
FINE = False
FILL_EVERY = 4


class Engine:
    def __init__(self, name, skip_self=False):
        self.name = name
        self.ops = []
        self.n = 0
        self.waited = {}
        self.refd = set()
        self.skip_self = skip_self
        self.sem = None
        self._sorted = None

    def resolve(self, i):
        if self._sorted is None:
            self._sorted = sorted(self.refd)
        return bisect.bisect_right(self._sorted, i)


class DSem:
    def __init__(self, name):
        self.name = name
        self.n = 0
        self.sem = None

    def resolve(self, i):
        return 16 * i


class Res:
    def __init__(self, name="", excl=False):
        self.name = name
        self.w = {}
        self.r = {}
        self.excl = excl


def emit(eng, fn, reads=(), writes=(), dsem=None):
    need = {}
    for r in reads:
        for o, i in r.w.items():
            if need.get(o, 0) < i:
                need[o] = i
        if r.excl:
            for o, i in r.r.items():
                if o is not eng and need.get(o, 0) < i:
                    need[o] = i
    for w in writes:
        for o, i in w.w.items():
            if need.get(o, 0) < i:
                need[o] = i
        for o, i in w.r.items():
            if need.get(o, 0) < i:
                need[o] = i
    for o, i in need.items():
        if o is eng and eng.skip_self:
            continue
        if eng.waited.get(o, 0) >= i:
            continue
        eng.waited[o] = i
        if isinstance(o, Engine):
            o.refd.add(i)
        eng.ops.append(("wait", o, i))
    if dsem is None:
        eng.n += 1
        tok = (eng, eng.n)
        eng.ops.append(("op", fn, eng.n))
    else:
        dsem.n += 1
        tok = (dsem, dsem.n)
        eng.ops.append(("dma", fn, dsem))
    for r in reads:
        if r.r.get(tok[0], 0) < tok[1]:
            r.r[tok[0]] = tok[1]
    for w in writes:
        w.w[tok[0]] = tok[1]
        w.r = {}
    return tok


def wait_tok(eng, obj, idx):
    if eng.waited.get(obj, 0) >= idx:
        return
    eng.waited[obj] = idx
    if isinstance(obj, Engine):
        obj.refd.add(idx)
    eng.ops.append(("wait", obj, idx))


def replay(eng, h):
    for op in eng.ops:
        if op[0] == "wait":
            h.wait_ge(op[1].sem, op[1].resolve(op[2]))
        elif op[0] == "op":
            inst = op[1](h)
            if op[2] in eng.refd:
                inst.then_inc(eng.sem, 1)
        else:
            inst = op[1](h)
            inst.then_inc(op[2].sem, 16)


def chain_gens(*gens):
    for g in gens:
        for _ in g:
            yield


def MM(out, lhsT, rhs, start=True, stop=True, sgc=False):
    return lambda t: t.matmul(out, lhsT=lhsT, rhs=rhs, start=start, stop=stop, skip_group_check=sgc)


def TP(out, in_, idn):
    return lambda t: t.transpose(out, in_, idn)


def ACTV(out, in_, func, bias=0.0, scale=1.0, accum_out=None):
    if accum_out is None:
        return lambda a: a.activation(out=out, in_=in_, func=func, bias=bias, scale=scale)
    return lambda a: a.activation(out=out, in_=in_, func=func, bias=bias, scale=scale, accum_out=accum_out)


def TT(out, in0, in1, op):
    return lambda v: v.tensor_tensor(out=out, in0=in0, in1=in1, op=op)


def TS(out, in0, s1, op0):
    return lambda v: v.tensor_scalar(out=out, in0=in0, scalar1=s1, scalar2=None, op0=op0)


def STT(out, in0, scalar, in1, op0, op1):
    return lambda v: v.scalar_tensor_tensor(out=out, in0=in0, scalar=scalar, in1=in1, op0=op0, op1=op1)


def CP(out, in_):
    return lambda v: v.tensor_copy(out=out, in_=in_)


def MS(ap, val):
    return lambda v: v.memset(ap, val)


def RC(out, in_):
    return lambda v: v.reciprocal(out=out, in_=in_)


def DMA(out, in_):
    return lambda q: q.dma_start(out=out, in_=in_)

def build_program():
    nc = bass.Bass("TRN2", target_bir_lowering=False)

    def din(name, shape):
        return nc.dram_tensor(name, list(shape), F32, kind="ExternalInput")

    x_d = din("x", [NB, S, D])
    mem_d = din("mem", [NB, MEM, D])
    wpair_d = din("wpair", [12, 128, WSLOT])
    wg8_d = din("wg8", [8, 128, 2048])
    wmq2_d = din("wmq2", [2, 128, 2048])
    wmkv4_d = din("wmkv4", [4, 128, 2048])
    wfl_d = din("wfl", [128, 96])
    wout_d = din("wout", [128, 16384])
    gf_d = din("gf", [128, 1024])
    g1T_d = din("g1T", [128, 8])
    gmT_d = din("gmT", [128, 8])
    bfor_d = din("bfor", [128, 192])
    cbf_d = din("cbf", [128, 128 * 4 + 2048])
    cf_d = din("cf", [128, 128 * 3])
    rope_d = din("rope", [128, 4096])
    out_d = nc.dram_tensor("out", [NB, S, D], F32, kind="ExternalOutput")

    PE = Engine("pe", skip_self=True)
    ACT = Engine("act")
    DVE = Engine("dve")
    POOL = Engine("pool")
    SP = Engine("sp")
    engines = [PE, ACT, DVE, POOL, SP]
    dsems = []

    def mkdsem(name):
        d = DSem(name)
        dsems.append(d)
        return d

    es = ExitStack()
    with es:
        def sb(name, free, dt):
            return es.enter_context(nc.sbuf_tensor("s_" + name, [128, free], dt))

        def ps(name, free, dt):
            return es.enter_context(nc.psum_tensor("p_" + name, [128, free], dt))

        big = sb("big", 16384, BF16)
        G = sb("G", 16 * 2048, BF16)
        Qa = [sb("Qa%d" % i, 2048, BF16) for i in range(2)]
        Qb = [sb("Qb%d" % i, 2048, BF16) for i in range(2)]
        Ka = [sb("Ka%d" % i, 2048, BF16) for i in range(2)]
        Kb = [sb("Kb%d" % i, 2048, BF16) for i in range(2)]
        VPW = 66
        Vp = [sb("Vp%d" % i, 16 * 2 * VPW, BF16) for i in range(2)]
        wsl = [sb("wsl0", WSLOT, BF16), sb("wsl1", WSLOT, BF16)]
        wfl = sb("wfl", 96, BF16)
        xt = [sb("xt0", 1024, F32), sb("xt1", 1024, F32)]
        xs = [sb("xs0", 1024, F32), sb("xs1", 1024, F32)]
        VT = xt[1].bitcast(BF16)
        tAs = [xs[0][:, 0:512], xs[0][:, 512:1024]]
        tU = xt[0][:, 512:1024]
        fl = xs[0][:, 0:192]
        Lg = xs[0][:, 192:384]
        tsb = xs[0][:, 384:576]
        pre = xs[0][:, 576:768]
        ctm = xs[0][:, 768:960]
        gf = sb("gf", 1024, F32)
        g1T = sb("g1T", 8, F32)
        gmT = sb("gmT", 8, F32)
        bfor = sb("bfor", 192, F32)
        PT = [sb("PT%d" % i, 512, BF16) for i in range(5)]
        cbf = sb("cbf", 128 * 4 + 2048, BF16)
        zer = sb("zer", 512, BF16)
        epsb = sb("epsb", 1, F32)
        oneb = sb("oneb", 1, F32)
        cf = sb("cf", 384, F32)
        ropeb = [sb("ropeb0", 1024, F32), sb("ropeb1", 1024, F32)]
        MK = sb("MK", 4 * 256, BF16)
        MVW = 130
        MVp = sb("MVp", 2 * 4 * MVW, BF16)
        tbs = [sb("tb0", 512, BF16), sb("tb1", 512, BF16)]
        yT = sb("yT", 2048, BF16)
        XW = 76
        XQ = sb("XQ", 16 * XW, BF16)
        XK = sb("XK", 16 * XW, BF16)
        P3 = sb("P3", 192 * 3, BF16)
        st = [sb("st%d" % i, 8, F32) for i in range(2)]
        rd = [sb("rd%d" % i, 4, F32) for i in range(2)]

        PJ = [ps("PJ0", 512, F32), ps("PJ1", 512, F32)]
        STp = [ps("ST%d" % i, 512, F32) for i in range(4)]
        OA = [ps("OA0", 512, F32), ps("OA1", 512, F32)]
        PJb = [t.bitcast(BF16) for t in PJ]

        R = lambda n: Res(n)
        r_hT = [R("hT%d" % i) for i in range(4)]
        r_wout = R("wout")
        r_G = [[R("G") for _ in range(16)] for _ in range(16)]
        r_Qa = [R("Qa0"), R("Qa1")]
        r_Qb = [R("Qb0"), R("Qb1")]
        r_Ka = [R("Ka0"), R("Ka1")]
        r_Kb = [R("Kb0"), R("Kb1")]
        r_Vp = [R("Vp0"), R("Vp1")]
        r_wsl = [R("wsl0"), R("wsl1")]
        r_xt = [R("xt0"), R("xt1")]
        r_xs = [R("xs0"), R("xs1")]
        r_VT = r_xt[1]
        r_sc = r_xs[0]
        r_const = R("const")
        r_zer = R("zer")
        r_PT = [R("PT") for _ in range(5)]
        r_rope = [R("rope0"), R("rope1")]
        r_MK, r_MVp = R("MK"), R("MVp")
        r_tbs = [R("tb0"), R("tb1")]
        r_tA = [R("tA0"), R("tA1")]
        r_tU = R("tU")
        r_yT = R("yT")
        r_X = R("X")
        r_P3 = R("P3")
        r_st = [R("st0"), R("st1")]
        r_rd = [R("rd0"), R("rd1")]
        r_PJ = [Res("PJ0", True), Res("PJ1", True)]
        r_ST = [Res("ST%d" % i, True) for i in range(4)]
        r_OA = [Res("OA0", True), Res("OA1", True)]

        d_x = [mkdsem("dx0"), mkdsem("dx1")]
        d_w = [mkdsem("dw0"), mkdsem("dw1")]
        d_r = [mkdsem("dr0"), mkdsem("dr1")]
        d_o = [mkdsem("do0"), mkdsem("do1")]
        d_wo = mkdsem("dwo")

        def AP(t, off, dims):
            return bass.AP(t, off, [list(d) for d in dims])

        ident = cbf[:, 0:128]
        negmask = cbf[:, 128:256]
        pswap = cbf[:, 256:384]
        WM0 = 384
        farmask = cbf[:, 384 + 2048:384 + 2048 + 128]
        identf = cf[:, 0:128]
        negtri = cf[:, 128:256]
        negones = cf[:, 256:384]

        def load_const(q, dst_ap, src_ap, name):
            d = mkdsem("dc_" + name)
            emit(q, DMA(dst_ap, src_ap), dsem=d)
            r_const.w[d] = 1

        load_const(POOL, cbf[:], cbf_d.ap(), "cbf")
        load_const(POOL, wfl[:], wfl_d.ap(), "wfl")
        load_const(SP, cf[:], cf_d.ap(), "cf")
        load_const(SP, gf[:], gf_d.ap(), "gf")
        load_const(SP, g1T[:], g1T_d.ap(), "g1T")
        load_const(SP, gmT[:], gmT_d.ap(), "gmT")
        load_const(SP, bfor[:], bfor_d.ap(), "bfor")
        emit(DVE, MS(zer[:], 0.0), writes=[r_zer])
        emit(DVE, MS(epsb[:], EPS), writes=[r_zer])
        emit(DVE, MS(oneb[:], 1.0), writes=[r_zer])
        for i in range(2):
            emit(DVE, MS(Qa[i][:], 0.0), writes=[r_Qa[i]])
            emit(DVE, MS(Qb[i][:], 0.0), writes=[r_Qb[i]])
            emit(DVE, MS(Ka[i][:], 0.0), writes=[r_Ka[i]])
            emit(DVE, MS(Kb[i][:], 0.0), writes=[r_Kb[i]])
            emit(DVE, MS(AP(Vp[i], 64, [[16 * 2 * VPW, 128], [VPW, 32], [1, 1]]), 1.0), writes=[r_Vp[i]])
        emit(DVE, MS(AP(MVp, 128, [[2 * 4 * MVW, 128], [MVW, 8], [1, 1]]), 1.0), writes=[r_MVp])
        emit(DVE, MS(XQ[:], 0.0), writes=[r_X])
        emit(DVE, MS(XK[:], 0.0), writes=[r_X])
        for off in (3, 73):
            emit(DVE, MS(AP(XQ, off, [[16 * XW, 128], [XW, 16], [1, 3]]), 1.0), writes=[r_X])
        for off in (0, 70):
            emit(DVE, MS(AP(XK, off, [[16 * XW, 128], [XW, 16], [1, 3]]), 1.0), writes=[r_X])

        cnt = dict(pj=0, x=0, w=0, st=0, pt=0, oa=0, rope=0)

        def nxt(key, mod):
            k = cnt[key] % mod
            cnt[key] += 1
            return k

        def load_weights(src_ap, ncols_total):
            k = nxt("w", 2)
            emit(POOL, DMA(wsl[k][:, 0:ncols_total], src_ap), writes=[r_wsl[k]], dsem=d_w[k])
            return k

        def rms_stats(src_ap, k, r_src):
            emit(ACT, ACTV(xs[k][:], src_ap, AF.Square, accum_out=st[k][:, 0:1]),
                 reads=[r_src], writes=[r_xs[k], r_st[k]] + (r_tA if k == 0 else []))
            emit(ACT, ACTV(st[k][:, 1:2], st[k][:, 0:1], AF.Ln, bias=epsb[:, 0:1], scale=1.0 / D),
                 reads=[r_st[k], r_zer], writes=[r_st[k]])
            emit(ACT, ACTV(st[k][:, 2:3], st[k][:, 1:2], AF.Exp, scale=-0.5),
                 reads=[r_st[k]], writes=[r_st[k]])

        def norm_part1(src_rows):
            k = nxt("x", 2)
            emit(SP, DMA(xt[k][:], src_rows), writes=[r_xt[k]] + ([r_tU] if k == 0 else []), dsem=d_x[k])
            rms_stats(xt[k][:], k, r_xt[k])
            emit(DVE, TS(xs[k][:], xt[k][:], st[k][:, 2:3], ALU.mult),
                 reads=[r_xt[k], r_st[k]], writes=[r_xs[k]])
            return k

        def norm_part2(k, gT, dst_t, dst_free, dst_stride, dst_off, r_dst):
            for half in range(2):
                tk = nxt("pj", 2)
                for q in range(4):
                    c = half * 4 + q
                    emit(PE, TP(PJ[tk][:, q * 128:(q + 1) * 128], xs[k][:, c * 128:(c + 1) * 128], identf),
                         reads=[r_xs[k], r_const], writes=[r_PJ[tk]])
                emit(DVE, TT(AP(dst_t, half * 4 * dst_stride + dst_off, [[dst_free, 128], [dst_stride, 4], [1, 128]]),
                             AP(PJ[tk], 0, [[512, 128], [128, 4], [1, 128]]),
                             AP(gT, half * 4, [[8, 128], [1, 4], [0, 128]]), ALU.mult),
                     reads=[r_PJ[tk], r_const], writes=[r_dst])

        def attention(kind, qT, kT, kbase, v_of, E, nblk, gcol, grp, r_q, r_k, r_v, filler=None, fill_every=FILL_EVERY):
            W = 512 // nblk
            nchunk = NT // nblk
            tiles = []
            for qc in range(nchunk):
                i0 = qc * nblk
                js = [0, 1] if kind == "mem" else list(range(i0 + nblk))
                for j in js:
                    ifirst = i0 if kind == "mem" else max(i0, j)
                    tiles.append(dict(qc=qc, j=j, ifirst=ifirst, ilast=i0 + nblk - 1, i0=i0,
                                      first=(j == js[0]), last=(j == js[-1])))
            state = {}

            def stage_A(tl):
                s = nxt("st", 4)
                tl["s"] = s
                if tl["first"]:
                    o = nxt("oa", 2)
                    state["o"] = o
                    emit(PE, MM(OA[o][:], zer[:, 0:128], zer[:, 0:512], start=True, stop=False, sgc=True),
                         reads=[r_zer], writes=[r_OA[o]])
                tl["o"] = state["o"]
                j = tl["j"]
                n0 = tl["ifirst"] * 128
                N = (tl["ilast"] - tl["ifirst"] + 1) * 128
                tl["N"] = N
                diag = (kind == "fox") and (tl["ifirst"] == j)
                far = (kind == "dil") and (tl["ifirst"] - j >= 5)
                tl["far"] = far
                emit(PE, MM(STp[s][:, 0:N], kT[:, kbase + j * 128:kbase + (j + 1) * 128], qT[:, n0:n0 + N],
                            start=True, stop=not (diag or far), sgc=True),
                     reads=[r_q, r_k], writes=[r_ST[s]])
                if far:
                    nb_ = N // 128
                    for bi in range(nb_):
                        emit(PE, MM(STp[s][:, bi * 128:(bi + 1) * 128], ident, farmask, start=False, stop=(bi == nb_ - 1), sgc=True),
                             reads=[r_const], writes=[r_ST[s]])
                if diag:
                    emit(PE, MM(STp[s][:, 0:128], ident, negmask, start=False, stop=True, sgc=True),
                         reads=[r_const], writes=[r_ST[s]])

            def stage_B(tl):
                s = tl["s"]
                p = nxt("pt", 5)
                tl["p"] = p
                N = tl["N"]
                j = tl["j"]
                if kind == "fox":
                    emit(ACT, ACTV(PT[p][:, 0:N], STp[s][:, 0:N], AF.Exp, scale=0.125),
                         reads=[r_ST[s]], writes=[r_PT[p]])
                elif kind == "dil":
                    emit(ACT, ACTV(PT[p][:, 0:N], STp[s][:, 0:N], AF.Exp, scale=0.125),
                         reads=[r_ST[s]], writes=[r_PT[p]])
                    d0 = tl["ifirst"] - j
                    if not tl["far"]:
                        emit(DVE, TT(PT[p][:, 0:N], PT[p][:, 0:N], cbf[:, WM0 + d0 * 128:WM0 + d0 * 128 + N], ALU.mult),
                             reads=[r_PT[p], r_const], writes=[r_PT[p]])
                else:
                    emit(ACT, ACTV(PT[p][:, 0:N], STp[s][:, 0:N], AF.Exp, scale=float(128.0 ** -0.5)),
                         reads=[r_ST[s]], writes=[r_PT[p]])

            def stage_C(tl):
                p = tl["p"]
                o = tl["o"]
                j = tl["j"]
                nb_ = tl["ilast"] - tl["ifirst"] + 1
                for bi, i in enumerate(range(tl["ifirst"], tl["ilast"] + 1)):
                    blk = i - tl["i0"]
                    lastmm = tl["last"] and (bi == nb_ - 1)
                    emit(PE, MM(OA[o][:, blk * W:blk * W + E + 1], PT[p][:, bi * 128:(bi + 1) * 128], v_of(j),
                                start=False, stop=lastmm, sgc=True),
                         reads=[r_PT[p], r_v], writes=[r_OA[o]])
                if tl["last"]:
                    emit(DVE, RC(rd[o][:, 0:nblk], AP(OA[o], E, [[512, 128], [W, nblk]])),
                         reads=[r_OA[o]], writes=[r_rd[o]])
                    for blk in range(nblk):
                        tt = tl["i0"] + blk
                        ga = G[:, tt * 2048 + gcol:tt * 2048 + gcol + E]
                        emit(DVE, STT(ga, OA[o][:, blk * W:blk * W + E], rd[o][:, blk:blk + 1], ga, ALU.mult, ALU.mult),
                             reads=[r_OA[o], r_rd[o], r_G[tt][grp]], writes=[r_G[tt][grp]])

            LA = 2 if kind == "mem" else 4
            for n in range(len(tiles) + LA):
                if n >= LA:
                    stage_C(tiles[n - LA])
                if n < len(tiles):
                    stage_A(tiles[n])
                    stage_B(tiles[n])
                if filler is not None and n % fill_every == fill_every - 1:
                    next(filler, None)

        def hT_ap(c, t0, n):
            return big[:, c * 2048 + t0:c * 2048 + t0 + n]

        def proj_pair(pp, slot):
            is_dil = pp >= 6
            wk = load_weights(wpair_d.ap()[pp], WSLOT)
            QA, QB, KA, KB, VP = Qa[slot], Qb[slot], Ka[slot], Kb[slot], Vp[slot]
            rQA, rQB, rKA, rKB, rVP = r_Qa[slot], r_Qb[slot], r_Ka[slot], r_Kb[slot], r_Vp[slot]
            if pp in (6, 7):
                emit(DVE, MS(QA[64:70, :], 0.0), writes=[rQA])
                emit(DVE, MS(KA[64:70, :], 0.0), writes=[rKA])
                emit(DVE, MS(QB[0:6, :], 0.0), writes=[rQB])
                emit(DVE, MS(KB[0:6, :], 0.0), writes=[rKB])
            units = []
            ropek = {}

            def mk_unit(tc, kind, u):
                cs = slice(tc * 512, (tc + 1) * 512)
                st_ = {}

                def P1():
                    if is_dil and kind == 0:
                        rk = nxt("rope", 2)
                        ropek[tc] = rk
                        emit(SP, DMA(AP(ropeb[rk], 0, [[1024, 128], [512, 2], [1, 512]]),
                                     AP(rope_d, tc * 512, [[4096, 128], [2048, 2], [1, 512]])),
                             writes=[r_rope[rk]], dsem=d_r[rk])
                    pk = nxt("pj", 2)
                    st_["pk"] = pk
                    for c in range(NCH):
                        emit(PE, MM(PJ[pk][:], wsl[wk][:, c * 384 + kind * 128:c * 384 + kind * 128 + 128],
                                    hT_ap(c, tc * 512, 512), start=(c == 0), stop=(c == NCH - 1)),
                             reads=[r_wsl[wk], r_hT[tc]], writes=[r_PJ[pk]])
                        if FINE and c % 2 == 1 and c < NCH - 1:
                            yield

                def P2():
                    pk = st_["pk"]
                    if kind == 2:
                        emit(ACT, ACTV(VT[:, cs], PJ[pk][:], AF.Copy), reads=[r_PJ[pk]], writes=[r_VT])
                    elif not is_dil:
                        if kind == 0:
                            emit(DVE, CP(QA[0:64, cs], PJ[pk][0:64, :]), reads=[r_PJ[pk]], writes=[rQA])
                            emit(DVE, CP(QB[64:128, cs], PJ[pk][64:128, :]), reads=[r_PJ[pk]], writes=[rQB])
                        else:
                            emit(DVE, CP(KA[0:64, cs], PJ[pk][0:64, :]), reads=[r_PJ[pk]], writes=[rKA])
                            emit(DVE, CP(KB[64:128, cs], PJ[pk][64:128, :]), reads=[r_PJ[pk]], writes=[rKB])
                    else:
                        a = u % 2
                        rk = ropek[tc]
                        emit(ACT, ACTV(tbs[a][:], PJ[pk][:], AF.Copy), reads=[r_PJ[pk]], writes=[r_tbs[a]])
                        emit(DVE, TT(tAs[a], PJ[pk][:], ropeb[rk][:, 0:512], ALU.mult),
                             reads=[r_PJ[pk], r_rope[rk]], writes=[r_tA[a], r_xs[0]])

                def P3():
                    if not is_dil or kind == 2:
                        return
                    a = u % 2
                    rk = ropek[tc]
                    tk = nxt("pj", 2)
                    emit(PE, MM(PJ[tk][:], pswap, tbs[a][:]), reads=[r_tbs[a], r_const], writes=[r_PJ[tk]])
                    emit(DVE, TT(tU, PJ[tk][:], ropeb[rk][:, 512:1024], ALU.mult),
                         reads=[r_PJ[tk], r_rope[rk]], writes=[r_tU, r_xt[0]])
                    tA0, tA1 = xs[0][0:64, a * 512:(a + 1) * 512], xs[0][64:128, a * 512:(a + 1) * 512]
                    tU0, tU1 = xt[0][0:64, 512:1024], xt[0][64:128, 512:1024]
                    if kind == 0:
                        emit(POOL, TT(QA[0:64, cs], tA0, tU0, ALU.add), reads=[r_tA[a], r_tU], writes=[rQA])
                        emit(POOL, TT(QB[64:128, cs], tA1, tU1, ALU.add), reads=[r_tA[a], r_tU], writes=[rQB])
                    else:
                        emit(POOL, TT(KA[0:64, cs], tA0, tU0, ALU.add), reads=[r_tA[a], r_tU], writes=[rKA])
                        emit(POOL, TT(KB[64:128, cs], tA1, tU1, ALU.add), reads=[r_tA[a], r_tU], writes=[rKB])

                return (P1, P2, P3)

            u = 0
            for tc in range(4):
                for kind in range(3):
                    units.append(mk_unit(tc, kind, u))
                    u += 1

            def mk_vt(h8):
                def P1():
                    tk = nxt("pj", 2)
                    for q in range(8):
                        tt = h8 * 8 + q
                        emit(PE, TP(PJb[tk][:, q * 128:(q + 1) * 128], VT[:, tt * 128:(tt + 1) * 128], ident),
                             reads=[r_VT, r_const], writes=[r_PJ[tk]])
                    emit(DVE, CP(AP(VP, h8 * 8 * 2 * VPW, [[16 * 2 * VPW, 128], [2 * VPW, 8], [VPW, 2], [1, 64]]),
                                 AP(PJb[tk], 0, [[1024, 128], [128, 8], [64, 2], [1, 64]])),
                         reads=[r_PJ[tk]], writes=[rVP])
                return (P1, None, None)

            def mk_aug(hl, which, h8):
                hidx = pp * 2 + hl
                qo, ko = (70, 73) if hl == 0 else (0, 3)
                c0, ncol, p0 = (6, 70, 64) if hl == 0 else (0, 6, 0)
                if which == 0:
                    X, dst, rdst = XQ, (QA if hl == 0 else QB), (rQA if hl == 0 else rQB)
                else:
                    X, dst, rdst = XK, (KA if hl == 0 else KB), (rKA if hl == 0 else rKB)

                def P1():
                    if h8 == 0:
                        if which == 0:
                            emit(DVE, CP(AP(XQ, qo, [[16 * XW, 128], [XW, 16], [1, 3]]),
                                         AP(P3, hidx * 3, [[576, 128], [36, 16], [1, 3]])),
                                 reads=[r_P3], writes=[r_X])
                        else:
                            emit(DVE, TS(AP(XK, ko, [[16 * XW, 128], [XW, 16], [1, 3]]),
                                         AP(P3, hidx * 3, [[576, 128], [36, 16], [1, 3]]), -1.0, ALU.mult),
                                 reads=[r_P3], writes=[r_X])
                    tk = nxt("pj", 2)
                    for q in range(8):
                        tt = h8 * 8 + q
                        emit(PE, TP(PJb[tk][0:ncol, q * 128:(q + 1) * 128],
                                    X[:, tt * XW + c0:tt * XW + c0 + ncol], ident),
                             reads=[r_X, r_const], writes=[r_PJ[tk]])
                    emit(DVE, CP(dst[p0:p0 + 6, h8 * 1024:(h8 + 1) * 1024], PJb[tk][p0:p0 + 6, 0:1024]),
                         reads=[r_PJ[tk]], writes=[rdst])
                return (P1, None, None)

            tail = [mk_vt(0), mk_vt(1)]
            if not is_dil:
                for hl in range(2):
                    for which in range(2):
                        for h8 in range(2):
                            tail.append(mk_aug(hl, which, h8))
            nu = len(units)
            for k in range(nu + 2):
                if 0 <= k - 2 < nu:
                    units[k - 2][2]()
                if 0 <= k - 1 < nu:
                    units[k - 1][1]()
                if k < nu:
                    for _ in units[k][0]():
                        yield
                yield
            for t_ in tail:
                t_[0]()
                yield

        for b in range(NB):
            mhT = yT
            jobs = [(x_d.ap()[b, tt * 128:(tt + 1) * 128, :], (g1T, big, 16384, 2048, tt * 128, r_hT[tt // 4])) for tt in range(NT)]
            jobs += [(mem_d.ap()[b, mt * 128:(mt + 1) * 128, :], (gmT, mhT, 2048, 256, mt * 128, r_yT)) for mt in range(2)]
            kprev = norm_part1(jobs[0][0])
            for ji in range(len(jobs)):
                knext = norm_part1(jobs[ji + 1][0]) if ji + 1 < len(jobs) else None
                norm_part2(kprev, *jobs[ji][1])
                kprev = knext
            def stage_c1():
                pk = nxt("pj", 2)
                for tt in range(NT):
                    for c in range(NCH):
                        emit(PE, MM(PJ[pk][:, tt * 12:(tt + 1) * 12], hT_ap(c, tt * 128, 128), wfl[:, c * 12:(c + 1) * 12],
                                    start=(c == 0), stop=(c == NCH - 1)),
                             reads=[r_const, r_hT[tt // 4]], writes=[r_PJ[pk]])
                emit(DVE, TT(fl, PJ[pk][:, 0:192], bfor[:], ALU.add), reads=[r_PJ[pk], r_const], writes=[r_sc] + r_tA)
                emit(ACT, ACTV(Lg, fl, AF.Exp, scale=-1.0), reads=[r_sc], writes=[r_sc])
                emit(ACT, ACTV(Lg, Lg, AF.Ln, bias=oneb[:, 0:1], scale=1.0), reads=[r_sc, r_zer], writes=[r_sc])

            def stage_c2():
                pk1 = nxt("pj", 2)
                emit(PE, MM(PJ[pk1][:, 0:192], negtri, Lg), reads=[r_sc, r_const], writes=[r_PJ[pk1]])
                pk2 = nxt("pj", 2)
                emit(PE, MM(PJ[pk2][:, 0:192], negones, Lg), reads=[r_sc, r_const], writes=[r_PJ[pk2]])
                emit(DVE, CP(tsb, PJ[pk2][:, 0:192]), reads=[r_PJ[pk2]], writes=[r_sc])
                emit(DVE, MS(pre[:, 0:12], 0.0), writes=[r_sc])
                for tt in range(1, NT):
                    emit(DVE, TT(pre[:, tt * 12:(tt + 1) * 12], pre[:, (tt - 1) * 12:tt * 12], tsb[:, (tt - 1) * 12:tt * 12], ALU.add),
                         reads=[r_sc], writes=[r_sc])
                emit(DVE, TT(ctm, PJ[pk1][:, 0:192], pre, ALU.add), reads=[r_PJ[pk1], r_sc], writes=[r_sc])

            def stage_c3():
                p3 = lambda t, k: AP(t, k, [[576, 128], [3, 192]])
                emit(DVE, TS(p3(P3, 0), ctm, 8.0, ALU.mult), reads=[r_sc], writes=[r_P3])
                emit(DVE, STT(fl, ctm, 8.0, p3(P3, 0), ALU.mult, ALU.subtract), reads=[r_sc, r_P3], writes=[r_sc])
                emit(DVE, CP(p3(P3, 1), fl), reads=[r_sc], writes=[r_P3])
                emit(DVE, TT(Lg, fl, p3(P3, 1), ALU.subtract), reads=[r_sc, r_P3], writes=[r_sc])
                emit(DVE, CP(p3(P3, 2), Lg), reads=[r_sc], writes=[r_P3])


            stage_c1()
            gen0 = None
            for g8 in range(8):
                wk = load_weights(wg8_d.ap()[g8], 2048)
                for t2 in range(NT // 2):
                    pk = nxt("pj", 2)
                    for hf in range(2):
                        tt = 2 * t2 + hf
                        for c in range(NCH):
                            emit(PE, MM(PJ[pk][:, hf * 256:(hf + 1) * 256], hT_ap(c, tt * 128, 128),
                                        wsl[wk][:, c * 256:(c + 1) * 256], start=(c == 0), stop=(c == NCH - 1)),
                                 reads=[r_wsl[wk], r_hT[tt // 4]], writes=[r_PJ[pk]])
                    emit(ACT, ACTV(AP(G, 2 * t2 * 2048 + g8 * 256, [[16 * 2048, 128], [2048, 2], [1, 256]]),
                                   AP(PJ[pk], 0, [[512, 128], [256, 2], [1, 256]]), AF.Silu),
                         reads=[r_PJ[pk]],
                         writes=[r_G[2 * t2][2 * g8], r_G[2 * t2][2 * g8 + 1], r_G[2 * t2 + 1][2 * g8], r_G[2 * t2 + 1][2 * g8 + 1]])
                    if gen0 is not None:
                        for _ in range(4 if FINE else 1):
                            next(gen0, None)
                if g8 == 0:
                    stage_c2()
                elif g8 == 1:
                    stage_c3()
                elif g8 == 5:
                    gen0 = proj_pair(0, 0)

            for _ in gen0:
                pass
            MQ = [Qa[0], Qb[0], Ka[0], Kb[0]]
            r_MQ = [r_Qa[0], r_Qb[0], r_Ka[0], r_Kb[0]]

            def proj_memkv():
                for k4 in range(4):
                    wk = load_weights(wmkv4_d.ap()[k4], 2048)
                    if k4 < 2:
                        for hl in range(2):
                            hh = 2 * k4 + hl
                            pk = nxt("pj", 2)
                            for c in range(NCH):
                                emit(PE, MM(PJ[pk][:, 0:256], wsl[wk][:, c * 256 + hl * 128:c * 256 + hl * 128 + 128],
                                            mhT[:, c * 256:(c + 1) * 256], start=(c == 0), stop=(c == NCH - 1)),
                                     reads=[r_wsl[wk], r_yT], writes=[r_PJ[pk]])
                            emit(DVE, CP(MK[:, hh * 256:(hh + 1) * 256], PJ[pk][:, 0:256]),
                                 reads=[r_PJ[pk]], writes=[r_MK])
                            yield
                    else:
                        h0 = 2 * (k4 - 2)
                        for mt in range(2):
                            pk = nxt("pj", 2)
                            for c in range(NCH):
                                emit(PE, MM(PJ[pk][:, 0:256], mhT[:, c * 256 + mt * 128:c * 256 + mt * 128 + 128],
                                            wsl[wk][:, c * 256:(c + 1) * 256], start=(c == 0), stop=(c == NCH - 1)),
                                     reads=[r_wsl[wk], r_yT], writes=[r_PJ[pk]])
                            emit(DVE, CP(AP(MVp, (mt * 4 + h0) * MVW, [[2 * 4 * MVW, 128], [MVW, 2], [1, 128]]),
                                         AP(PJ[pk], 0, [[512, 128], [128, 2], [1, 128]])),
                                 reads=[r_PJ[pk]], writes=[r_MVp])
                            yield

            def proj_mem():
                pend = None
                for q2 in range(2):
                    wk = load_weights(wmq2_d.ap()[q2], 2048)
                    for hl in range(2):
                        hh = q2 * 2 + hl
                        for tc in range(4):
                            pk = nxt("pj", 2)
                            for c in range(NCH):
                                emit(PE, MM(PJ[pk][:], wsl[wk][:, c * 256 + hl * 128:c * 256 + hl * 128 + 128],
                                            hT_ap(c, tc * 512, 512), start=(c == 0), stop=(c == NCH - 1)),
                                     reads=[r_wsl[wk], r_hT[tc]], writes=[r_PJ[pk]])
                                if FINE and c % 2 == 1 and c < NCH - 1:
                                    yield
                            if pend is not None:
                                pend()
                            if tc % 2 == 0:
                                pend = (lambda hh, tc, pk: lambda: emit(
                                    DVE, CP(MQ[hh][:, tc * 512:(tc + 1) * 512], PJ[pk][:]), reads=[r_PJ[pk]], writes=[r_MQ[hh]]))(hh, tc, pk)
                            else:
                                pend = (lambda hh, tc, pk: lambda: emit(
                                    ACT, ACTV(MQ[hh][:, tc * 512:(tc + 1) * 512], PJ[pk][:], AF.Copy), reads=[r_PJ[pk]], writes=[r_MQ[hh]]))(hh, tc, pk)
                            yield
                pend()
                yield

            for pp in range(12):
                slot = pp % 2
                is_dil = pp >= 6
                nxt_gen = proj_pair(pp + 1, (pp + 1) % 2) if pp + 1 < 12 else chain_gens(proj_memkv(), proj_mem())
                for hl in range(2):
                    hidx = (pp % 6) * 2 + hl
                    qT = Qa[slot] if hl == 0 else Qb[slot]
                    kT = Ka[slot] if hl == 0 else Kb[slot]
                    r_q = r_Qa[slot] if hl == 0 else r_Qb[slot]
                    r_k = r_Ka[slot] if hl == 0 else r_Kb[slot]
                    v_of = (lambda hl, VP: lambda j: VP[:, (j * 2 + hl) * VPW:(j * 2 + hl) * VPW + 65])(hl, Vp[slot])
                    if not is_dil:
                        attention("fox", qT, kT, 0, v_of, 64, 4, hidx * 64, pp, r_q, r_k, r_Vp[slot], filler=nxt_gen)
                    else:
                        attention("dil", qT, kT, 0, v_of, 64, 4, 768 + hidx * 64, pp, r_q, r_k, r_Vp[slot], filler=nxt_gen,
                                  fill_every=(3 if pp == 11 else FILL_EVERY))
                if nxt_gen is not None:
                    for _ in nxt_gen:
                        pass

            for q4 in range(4):
                emit(POOL, DMA(big[:, q4 * 4096:(q4 + 1) * 4096], wout_d.ap()[:, q4 * 4096:(q4 + 1) * 4096]),
                     writes=([r_wout] + r_hT) if q4 == 0 else [], dsem=d_wo)
            for r_ in [r_wout] + r_hT:
                r_.w[d_wo] = d_wo.n
            for hh in range(4):
                v_of = (lambda hh: lambda j: MVp[:, (j * 4 + hh) * MVW:(j * 4 + hh) * MVW + 129])(hh)
                attention("mem", MQ[hh], MK, hh * 256, v_of, 128, 2, 1536 + hh * 128, 12 + hh, r_MQ[hh], r_MK, r_MVp)
            if b + 1 < NB:
                for hh in range(4):
                    emit(POOL, MS(MQ[hh][:], 0.0), writes=[r_MQ[hh]])

            yTb = [yT, Qa[1]]
            r_yTb = [r_yT, r_Qa[1]]

            FT = [STp[0].bitcast(BF16), STp[1].bitcast(BF16), OA[0].bitcast(BF16), OA[1].bitcast(BF16)]
            r_FT = [r_ST[0], r_ST[1], r_OA[0], r_OA[1]]
            FM = [PJ[0], PJ[1], STp[2], STp[3]]
            r_FM = [r_PJ[0], r_PJ[1], r_ST[2], r_ST[3]]
            fcnt = dict(t=0, m=0)

            def f_transposes(tt):
                yk = tt % 2
                for h8 in range(2):
                    tk = fcnt["t"] % 4
                    fcnt["t"] += 1
                    for q in range(8):
                        c = h8 * 8 + q
                        emit(PE, TP(FT[tk][:, q * 128:(q + 1) * 128], G[:, tt * 2048 + c * 128:tt * 2048 + (c + 1) * 128], ident),
                             reads=[r_G[tt][c], r_const], writes=[r_FT[tk]])
                    emit(DVE, CP(yTb[yk][:, h8 * 1024:(h8 + 1) * 1024], FT[tk][:, 0:1024]),
                         reads=[r_FT[tk]], writes=[r_yTb[yk]])

            f_transposes(0)
            for tt in range(NT):
                yk = tt % 2
                k = nxt("x", 2)
                emit(SP, DMA(xt[k][:], x_d.ap()[b, tt * 128:(tt + 1) * 128, :]), writes=[r_xt[k]] + ([r_tU] if k == 0 else []), dsem=d_x[k])
                if tt + 1 < NT:
                    f_transposes(tt + 1)
                pks = []
                for half in range(2):
                    pk = fcnt["m"] % 4
                    fcnt["m"] += 1
                    pks.append(pk)
                    for c in range(16):
                        emit(PE, MM(FM[pk][:], yTb[yk][:, c * 128:(c + 1) * 128],
                                    big[:, c * 1024 + half * 512:c * 1024 + (half + 1) * 512],
                                    start=(c == 0), stop=(c == 15)),
                             reads=[r_yTb[yk], r_wout] + r_hT, writes=[r_FM[pk]])
                for half in range(2):
                    pk = pks[half]
                    emit(DVE, TT(xt[k][:, half * 512:(half + 1) * 512], FM[pk][:], xt[k][:, half * 512:(half + 1) * 512], ALU.add),
                         reads=[r_FM[pk], r_xt[k]], writes=[r_xt[k]])
                rms_stats(xt[k][:], k, r_xt[k])
                emit(DVE, STT(xs[k][:], xt[k][:], st[k][:, 2:3], gf[:], ALU.mult, ALU.mult),
                     reads=[r_xt[k], r_st[k], r_const], writes=[r_xs[k]])
                emit(SP, DMA(out_d.ap()[b, tt * 128:(tt + 1) * 128, :], xs[k][:]), reads=[r_xs[k]], dsem=d_o[k])
            if b + 1 < NB:
                emit(POOL, MS(Qa[1][:], 0.0), writes=[r_Qa[1]])

        for k in range(2):
            wait_tok(SP, d_o[k], d_o[k].n)

        for e in engines:
            e.sem = es.enter_context(nc.semaphore("sem_" + e.name))
        for d in dsems:
            d.sem = es.enter_context(nc.semaphore("dsem_" + d.name))
        with nc.Block() as block:
            @block.tensor
            def _(t):
                replay(PE, t)

            @block.scalar
            def _(a):
                replay(ACT, a)

            @block.vector
            def _(v):
                replay(DVE, v)

            @block.gpsimd
            def _(g):
                replay(POOL, g)

            @block.sync
            def _(s):
                replay(SP, s)
    return nc


def _chunked(w, ncols):
    return np.ascontiguousarray(w.reshape(8, 128, ncols).transpose(1, 0, 2).reshape(128, 8 * ncols))


def _constants():
    p = np.arange(128)
    ident = np.eye(128, dtype=np.float32)
    negmask = np.where(p[:, None] <= p[None, :], 0.0, -30000.0).astype(np.float32)
    pswap = np.zeros((128, 128), np.float32)
    for m in range(128):
        ml = m % 64
        if ml < 8:
            pswap[m + 8, m] = -1.0
        elif ml < 16:
            pswap[m - 8, m] = 1.0
    wmask = np.zeros((128, 16, 128), np.float32)
    for dlt in range(16):
        delta = 128 * dlt + p[None, :] - p[:, None]
        m1 = (delta >= 0) & (delta <= 128)
        m2 = (delta >= 0) & (delta % 4 == 0) & (delta <= 512)
        m3 = (delta >= 0) & (delta % 16 == 0) & (delta <= 2048)
        wmask[:, dlt, :] = m1.astype(np.float32) + m2 + m3
    farmask = np.where((p[None, :] - p[:, None]) % 16 == 0, 0.0, -30000.0).astype(np.float32)
    cbf = np.concatenate([ident, negmask, pswap, wmask.reshape(128, 2048), farmask], axis=1)
    negtri = -(p[:, None] <= p[None, :]).astype(np.float32)
    negones = -np.ones((128, 128), np.float32)
    cf = np.concatenate([ident, negtri, negones], axis=1)
    pos = np.arange(S, dtype=np.float32)
    inv_freq = (1.0 / (np.float32(500000.0) ** (np.arange(0, 16, 2, dtype=np.float32) / np.float32(16)))).astype(np.float32)
    ang = (pos[:, None] * inv_freq[None, :]).astype(np.float32)
    C = np.ones((128, S), np.float32)
    Sn = np.zeros((128, S), np.float32)
    for m in range(128):
        ml = m % 64
        if ml < 16:
            C[m] = np.cos(ang[:, ml % 8])
            Sn[m] = np.sin(ang[:, ml % 8])
    rope = np.concatenate([C, Sn], axis=1).astype(np.float32)
    return np.ascontiguousarray(cbf), np.ascontiguousarray(cf), np.ascontiguousarray(rope)


_PROGRAM = None


def kernel(x, mem, norm_g, w_in, b_forget, mem_norm_g, w_mem_kv, w_out, final_norm_g):
    global _PROGRAM
    x = np.asarray(x, dtype=np.float32)
    mem = np.asarray(mem, dtype=np.float32)
    w = np.asarray(w_in, dtype=np.float32)[0]
    sizes = [768] * 4 + [12] + [768] * 4 + [512] * 2
    offs = np.cumsum([0] + sizes)
    fq, fk, fv, fg, flg, dq, dk, dv, dg, mq, mg = [w[:, offs[i]:offs[i + 1]] for i in range(11)]
    wpair = np.zeros((12, 128, WSLOT), np.float32)
    for pp in range(6):
        cs = slice(pp * 128, (pp + 1) * 128)
        wpair[pp] = _chunked(np.concatenate([fq[:, cs], fk[:, cs], fv[:, cs]], axis=1), 384)
        wpair[6 + pp] = _chunked(np.concatenate([dq[:, cs], dk[:, cs], dv[:, cs]], axis=1), 384)
    gates = np.concatenate([fg, dg, mg], axis=1)
    wg8 = np.stack([_chunked(gates[:, i * 256:(i + 1) * 256], 256) for i in range(8)])
    wmq2 = np.stack([_chunked(mq[:, i * 256:(i + 1) * 256], 256) for i in range(2)])
    wkv = np.asarray(w_mem_kv, dtype=np.float32)[0]
    wmkv4 = np.stack([_chunked(wkv[:, i * 256:(i + 1) * 256], 256) for i in range(4)])
    wfl = _chunked(flg, 12)
    wo = np.asarray(w_out, dtype=np.float32)[0]
    wout = np.ascontiguousarray(wo.reshape(16, 128, 1024).transpose(1, 0, 2).reshape(128, 16384))
    gfb = np.ascontiguousarray(np.broadcast_to(np.asarray(final_norm_g, np.float32)[None, :], (128, 1024)))
    g1T = np.ascontiguousarray(np.asarray(norm_g, np.float32)[0].reshape(8, 128).T)
    gmT = np.ascontiguousarray(np.asarray(mem_norm_g, np.float32)[0].reshape(8, 128).T)
    bfor = np.ascontiguousarray(np.broadcast_to(np.asarray(b_forget, np.float32)[0][None, None, :], (128, 16, 12)).reshape(128, 192))
    cbf, cf, rope = _constants()

    if _PROGRAM is None:
        _PROGRAM = build_program()
    nc = _PROGRAM
    shared = dict(wpair=wpair, wg8=wg8, wmq2=wmq2, wmkv4=wmkv4, wfl=wfl, wout=wout, gf=gfb, g1T=g1T, gmT=gmT,
                  bfor=bfor, cbf=cbf, cf=cf, rope=rope)
    in_maps = []
    for c in range(NCORES):
        m = dict(shared)
        m["x"] = np.ascontiguousarray(x[c * NB:(c + 1) * NB])
        m["mem"] = np.ascontiguousarray(mem[c * NB:(c + 1) * NB])
        in_maps.append(m)
    res = run_bass_kernel_spmd(nc, in_maps, core_ids=list(range(NCORES)))
    out = np.concatenate([np.asarray(r["out"], dtype=np.float32) for r in res.results], axis=0)
    return out
```

```python
import bisect
from contextlib import ExitStack

import numpy as np
import concourse.bass as bass
import concourse.mybir as mybir
from concourse.bass_utils import run_bass_kernel_spmd

F32 = mybir.dt.float32
BF16 = mybir.dt.bfloat16
AF = mybir.ActivationFunctionType
ALU = mybir.AluOpType

NCORES = 8
NB = 2
S = 2048
D = 1024
NT = 16
NCH = 8
MEM = 256
EPS = 1e-6
WSLOT = 3072
FINE = False
FILL_EVERY = 4


class Engine:
    def __init__(self, name, skip_self=False):
        self.name = name
        self.ops = []
        self.n = 0
        self.waited = {}
        self.refd = set()
        self.skip_self = skip_self
        self.sem = None
        self._sorted = None

    def resolve(self, i):
        if self._sorted is None:
            self._sorted = sorted(self.refd)
        return bisect.bisect_right(self._sorted, i)


class DSem:
    def __init__(self, name):
        self.name = name
        self.n = 0
        self.sem = None

    def resolve(self, i):
        return 16 * i


class Res:
    def __init__(self, name="", excl=False):
        self.name = name
        self.w = {}
        self.r = {}
        self.excl = excl


def emit(eng, fn, reads=(), writes=(), dsem=None):
    need = {}
    for r in reads:
        for o, i in r.w.items():
            if need.get(o, 0) < i:
                need[o] = i
        if r.excl:
            for o, i in r.r.items():
                if o is not eng and need.get(o, 0) < i:
                    need[o] = i
    for w in writes:
        for o, i in w.w.items():
            if need.get(o, 0) < i:
                need[o] = i
        for o, i in w.r.items():
            if need.get(o, 0) < i:
                need[o] = i
    for o, i in need.items():
        if o is eng and eng.skip_self:
            continue
        if eng.waited.get(o, 0) >= i:
            continue
        eng.waited[o] = i
        if isinstance(o, Engine):
            o.refd.add(i)
        eng.ops.append(("wait", o, i))
    if dsem is None:
        eng.n += 1
        tok = (eng, eng.n)
        eng.ops.append(("op", fn, eng.n))
    else:
        dsem.n += 1
        tok = (dsem, dsem.n)
        eng.ops.append(("dma", fn, dsem))
    for r in reads:
        if r.r.get(tok[0], 0) < tok[1]:
            r.r[tok[0]] = tok[1]
    for w in writes:
        w.w[tok[0]] = tok[1]
        w.r = {}
    return tok


def wait_tok(eng, obj, idx):
    if eng.waited.get(obj, 0) >= idx:
        return
    eng.waited[obj] = idx
    if isinstance(obj, Engine):
        obj.refd.add(idx)
    eng.ops.append(("wait", obj, idx))


def replay(eng, h):
    for op in eng.ops:
        if op[0] == "wait":
            h.wait_ge(op[1].sem, op[1].resolve(op[2]))
        elif op[0] == "op":
            inst = op[1](h)
            if op[2] in eng.refd:
                inst.then_inc(eng.sem, 1)
        else:
            inst = op[1](h)
            inst.then_inc(op[2].sem, 16)


def chain_gens(*gens):
    for g in gens:
        for _ in g:
            yield


def MM(out, lhsT, rhs, start=True, stop=True, sgc=False):
    return lambda t: t.matmul(out, lhsT=lhsT, rhs=rhs, start=start, stop=stop, skip_group_check=sgc)


def TP(out, in_, idn):
    return lambda t: t.transpose(out, in_, idn)


def ACTV(out, in_, func, bias=0.0, scale=1.0, accum_out=None):
    if accum_out is None:
        return lambda a: a.activation(out=out, in_=in_, func=func, bias=bias, scale=scale)
    return lambda a: a.activation(out=out, in_=in_, func=func, bias=bias, scale=scale, accum_out=accum_out)


def TT(out, in0, in1, op):
    return lambda v: v.tensor_tensor(out=out, in0=in0, in1=in1, op=op)


def TS(out, in0, s1, op0):
    return lambda v: v.tensor_scalar(out=out, in0=in0, scalar1=s1, scalar2=None, op0=op0)


def STT(out, in0, scalar, in1, op0, op1):
    return lambda v: v.scalar_tensor_tensor(out=out, in0=in0, scalar=scalar, in1=in1, op0=op0, op1=op1)


def CP(out, in_):
    return lambda v: v.tensor_copy(out=out, in_=in_)


def MS(ap, val):
    return lambda v: v.memset(ap, val)


def RC(out, in_):
    return lambda v: v.reciprocal(out=out, in_=in_)


def DMA(out, in_):
    return lambda q: q.dma_start(out=out, in_=in_)

def build_program():
    nc = bass.Bass("TRN2", target_bir_lowering=False)

    def din(name, shape):
        return nc.dram_tensor(name, list(shape), F32, kind="ExternalInput")

    x_d = din("x", [NB, S, D])
    mem_d = din("mem", [NB, MEM, D])
    wpair_d = din("wpair", [12, 128, WSLOT])
    wg8_d = din("wg8", [8, 128, 2048])
    wmq2_d = din("wmq2", [2, 128, 2048])
    wmkv4_d = din("wmkv4", [4, 128, 2048])
    wfl_d = din("wfl", [128, 96])
    wout_d = din("wout", [128, 16384])
    gf_d = din("gf", [128, 1024])
    g1T_d = din("g1T", [128, 8])
    gmT_d = din("gmT", [128, 8])
    bfor_d = din("bfor", [128, 192])
    cbf_d = din("cbf", [128, 128 * 4 + 2048])
    cf_d = din("cf", [128, 128 * 3])
    rope_d = din("rope", [128, 4096])
    out_d = nc.dram_tensor("out", [NB, S, D], F32, kind="ExternalOutput")

    PE = Engine("pe", skip_self=True)
    ACT = Engine("act")
    DVE = Engine("dve")
    POOL = Engine("pool")
    SP = Engine("sp")
    engines = [PE, ACT, DVE, POOL, SP]
    dsems = []

    def mkdsem(name):
        d = DSem(name)
        dsems.append(d)
        return d

    es = ExitStack()
    with es:
        def sb(name, free, dt):
            return es.enter_context(nc.sbuf_tensor("s_" + name, [128, free], dt))

        def ps(name, free, dt):
            return es.enter_context(nc.psum_tensor("p_" + name, [128, free], dt))

        big = sb("big", 16384, BF16)
        G = sb("G", 16 * 2048, BF16)
        Qa = [sb("Qa%d" % i, 2048, BF16) for i in range(2)]
        Qb = [sb("Qb%d" % i, 2048, BF16) for i in range(2)]
        Ka = [sb("Ka%d" % i, 2048, BF16) for i in range(2)]
        Kb = [sb("Kb%d" % i, 2048, BF16) for i in range(2)]
        VPW = 66
        Vp = [sb("Vp%d" % i, 16 * 2 * VPW, BF16) for i in range(2)]
        wsl = [sb("wsl0", WSLOT, BF16), sb("wsl1", WSLOT, BF16)]
        wfl = sb("wfl", 96, BF16)
        xt = [sb("xt0", 1024, F32), sb("xt1", 1024, F32)]
        xs = [sb("xs0", 1024, F32), sb("xs1", 1024, F32)]
        VT = xt[1].bitcast(BF16)
        tAs = [xs[0][:, 0:512], xs[0][:, 512:1024]]
        tU = xt[0][:, 512:1024]
        fl = xs[0][:, 0:192]
        Lg = xs[0][:, 192:384]
        tsb = xs[0][:, 384:576]
        pre = xs[0][:, 576:768]
        ctm = xs[0][:, 768:960]
        gf = sb("gf", 1024, F32)
        g1T = sb("g1T", 8, F32)
        gmT = sb("gmT", 8, F32)
        bfor = sb("bfor", 192, F32)
        PT = [sb("PT%d" % i, 512, BF16) for i in range(5)]
        cbf = sb("cbf", 128 * 4 + 2048, BF16)
        zer = sb("zer", 512, BF16)
        epsb = sb("epsb", 1, F32)
        oneb = sb("oneb", 1, F32)
        cf = sb("cf", 384, F32)
        ropeb = [sb("ropeb0", 1024, F32), sb("ropeb1", 1024, F32)]
        MK = sb("MK", 4 * 256, BF16)
        MVW = 130
        MVp = sb("MVp", 2 * 4 * MVW, BF16)
        tbs = [sb("tb0", 512, BF16), sb("tb1", 512, BF16)]
        yT = sb("yT", 2048, BF16)
        XW = 76
        XQ = sb("XQ", 16 * XW, BF16)
        XK = sb("XK", 16 * XW, BF16)
        P3 = sb("P3", 192 * 3, BF16)
        st = [sb("st%d" % i, 8, F32) for i in range(2)]
        rd = [sb("rd%d" % i, 4, F32) for i in range(2)]

        PJ = [ps("PJ0", 512, F32), ps("PJ1", 512, F32)]
        STp = [ps("ST%d" % i, 512, F32) for i in range(4)]
        OA = [ps("OA0", 512, F32), ps("OA1", 512, F32)]
        PJb = [t.bitcast(BF16) for t in PJ]

        R = lambda n: Res(n)
        r_hT = [R("hT%d" % i) for i in range(4)]
        r_wout = R("wout")
        r_G = [[R("G") for _ in range(16)] for _ in range(16)]
        r_Qa = [R("Qa0"), R("Qa1")]
        r_Qb = [R("Qb0"), R("Qb1")]
        r_Ka = [R("Ka0"), R("Ka1")]
        r_Kb = [R("Kb0"), R("Kb1")]
        r_Vp = [R("Vp0"), R("Vp1")]
        r_wsl = [R("wsl0"), R("wsl1")]
        r_xt = [R("xt0"), R("xt1")]
        r_xs = [R("xs0"), R("xs1")]
        r_VT = r_xt[1]
        r_sc = r_xs[0]
        r_const = R("const")
        r_zer = R("zer")
        r_PT = [R("PT") for _ in range(5)]
        r_rope = [R("rope0"), R("rope1")]
        r_MK, r_MVp = R("MK"), R("MVp")
        r_tbs = [R("tb0"), R("tb1")]
        r_tA = [R("tA0"), R("tA1")]
        r_tU = R("tU")
        r_yT = R("yT")
        r_X = R("X")
        r_P3 = R("P3")
        r_st = [R("st0"), R("st1")]
        r_rd = [R("rd0"), R("rd1")]
        r_PJ = [Res("PJ0", True), Res("PJ1", True)]
        r_ST = [Res("ST%d" % i, True) for i in range(4)]
        r_OA = [Res("OA0", True), Res("OA1", True)]

        d_x = [mkdsem("dx0"), mkdsem("dx1")]
        d_w = [mkdsem("dw0"), mkdsem("dw1")]
        d_r = [mkdsem("dr0"), mkdsem("dr1")]
        d_o = [mkdsem("do0"), mkdsem("do1")]
        d_wo = mkdsem("dwo")

        def AP(t, off, dims):
            return bass.AP(t, off, [list(d) for d in dims])

        ident = cbf[:, 0:128]
        negmask = cbf[:, 128:256]
        pswap = cbf[:, 256:384]
        WM0 = 384
        farmask = cbf[:, 384 + 2048:384 + 2048 + 128]
        identf = cf[:, 0:128]
        negtri = cf[:, 128:256]
        negones = cf[:, 256:384]

        def load_const(q, dst_ap, src_ap, name):
            d = mkdsem("dc_" + name)
            emit(q, DMA(dst_ap, src_ap), dsem=d)
            r_const.w[d] = 1

        load_const(POOL, cbf[:], cbf_d.ap(), "cbf")
        load_const(POOL, wfl[:], wfl_d.ap(), "wfl")
        load_const(SP, cf[:], cf_d.ap(), "cf")
        load_const(SP, gf[:], gf_d.ap(), "gf")
        load_const(SP, g1T[:], g1T_d.ap(), "g1T")
        load_const(SP, gmT[:], gmT_d.ap(), "gmT")
        load_const(SP, bfor[:], bfor_d.ap(), "bfor")
        emit(DVE, MS(zer[:], 0.0), writes=[r_zer])
        emit(DVE, MS(epsb[:], EPS), writes=[r_zer])
        emit(DVE, MS(oneb[:], 1.0), writes=[r_zer])
        for i in range(2):
            emit(DVE, MS(Qa[i][:], 0.0), writes=[r_Qa[i]])
            emit(DVE, MS(Qb[i][:], 0.0), writes=[r_Qb[i]])
            emit(DVE, MS(Ka[i][:], 0.0), writes=[r_Ka[i]])
            emit(DVE, MS(Kb[i][:], 0.0), writes=[r_Kb[i]])
            emit(DVE, MS(AP(Vp[i], 64, [[16 * 2 * VPW, 128], [VPW, 32], [1, 1]]), 1.0), writes=[r_Vp[i]])
        emit(DVE, MS(AP(MVp, 128, [[2 * 4 * MVW, 128], [MVW, 8], [1, 1]]), 1.0), writes=[r_MVp])
        emit(DVE, MS(XQ[:], 0.0), writes=[r_X])
        emit(DVE, MS(XK[:], 0.0), writes=[r_X])
        for off in (3, 73):
            emit(DVE, MS(AP(XQ, off, [[16 * XW, 128], [XW, 16], [1, 3]]), 1.0), writes=[r_X])
        for off in (0, 70):
            emit(DVE, MS(AP(XK, off, [[16 * XW, 128], [XW, 16], [1, 3]]), 1.0), writes=[r_X])

        cnt = dict(pj=0, x=0, w=0, st=0, pt=0, oa=0, rope=0)

        def nxt(key, mod):
            k = cnt[key] % mod
            cnt[key] += 1
            return k

        def load_weights(src_ap, ncols_total):
            k = nxt("w", 2)
            emit(POOL, DMA(wsl[k][:, 0:ncols_total], src_ap), writes=[r_wsl[k]], dsem=d_w[k])
            return k

        def rms_stats(src_ap, k, r_src):
            emit(ACT, ACTV(xs[k][:], src_ap, AF.Square, accum_out=st[k][:, 0:1]),
                 reads=[r_src], writes=[r_xs[k], r_st[k]] + (r_tA if k == 0 else []))
            emit(ACT, ACTV(st[k][:, 1:2], st[k][:, 0:1], AF.Ln, bias=epsb[:, 0:1], scale=1.0 / D),
                 reads=[r_st[k], r_zer], writes=[r_st[k]])
            emit(ACT, ACTV(st[k][:, 2:3], st[k][:, 1:2], AF.Exp, scale=-0.5),
                 reads=[r_st[k]], writes=[r_st[k]])

        def norm_part1(src_rows):
            k = nxt("x", 2)
            emit(SP, DMA(xt[k][:], src_rows), writes=[r_xt[k]] + ([r_tU] if k == 0 else []), dsem=d_x[k])
            rms_stats(xt[k][:], k, r_xt[k])
            emit(DVE, TS(xs[k][:], xt[k][:], st[k][:, 2:3], ALU.mult),
                 reads=[r_xt[k], r_st[k]], writes=[r_xs[k]])
            return k

        def norm_part2(k, gT, dst_t, dst_free, dst_stride, dst_off, r_dst):
            for half in range(2):
                tk = nxt("pj", 2)
                for q in range(4):
                    c = half * 4 + q
                    emit(PE, TP(PJ[tk][:, q * 128:(q + 1) * 128], xs[k][:, c * 128:(c + 1) * 128], identf),
                         reads=[r_xs[k], r_const], writes=[r_PJ[tk]])
                emit(DVE, TT(AP(dst_t, half * 4 * dst_stride + dst_off, [[dst_free, 128], [dst_stride, 4], [1, 128]]),
                             AP(PJ[tk], 0, [[512, 128], [128, 4], [1, 128]]),
                             AP(gT, half * 4, [[8, 128], [1, 4], [0, 128]]), ALU.mult),
                     reads=[r_PJ[tk], r_const], writes=[r_dst])

        def attention(kind, qT, kT, kbase, v_of, E, nblk, gcol, grp, r_q, r_k, r_v, filler=None, fill_every=FILL_EVERY):
            W = 512 // nblk
            nchunk = NT // nblk
            tiles = []
            for qc in range(nchunk):
                i0 = qc * nblk
                js = [0, 1] if kind == "mem" else list(range(i0 + nblk))
                for j in js:
                    ifirst = i0 if kind == "mem" else max(i0, j)
                    tiles.append(dict(qc=qc, j=j, ifirst=ifirst, ilast=i0 + nblk - 1, i0=i0,
                                      first=(j == js[0]), last=(j == js[-1])))
            state = {}

            def stage_A(tl):
                s = nxt("st", 4)
                tl["s"] = s
                if tl["first"]:
                    o = nxt("oa", 2)
                    state["o"] = o
                tl["o"] = state["o"]
                j = tl["j"]
                n0 = tl["ifirst"] * 128
                N = (tl["ilast"] - tl["ifirst"] + 1) * 128
                tl["N"] = N
                diag = (kind == "fox") and (tl["ifirst"] == j)
                far = (kind == "dil") and (tl["ifirst"] - j >= 5)
                tl["far"] = far
                emit(PE, MM(STp[s][:, 0:N], kT[:, kbase + j * 128:kbase + (j + 1) * 128], qT[:, n0:n0 + N],
                            start=True, stop=not (diag or far), sgc=True),
                     reads=[r_q, r_k], writes=[r_ST[s]])
                if far:
                    nb_ = N // 128
                    for bi in range(nb_):
                        emit(PE, MM(STp[s][:, bi * 128:(bi + 1) * 128], ident, farmask, start=False, stop=(bi == nb_ - 1), sgc=True),
                             reads=[r_const], writes=[r_ST[s]])
                if diag:
                    emit(PE, MM(STp[s][:, 0:128], ident, negmask, start=False, stop=True, sgc=True),
                         reads=[r_const], writes=[r_ST[s]])

            def stage_B(tl):
                s = tl["s"]
                p = nxt("pt", 5)
                tl["p"] = p
                N = tl["N"]
                j = tl["j"]
                if kind == "fox":
                    emit(ACT, ACTV(PT[p][:, 0:N], STp[s][:, 0:N], AF.Exp, scale=0.125),
                         reads=[r_ST[s]], writes=[r_PT[p]])
                elif kind == "dil":
                    emit(ACT, ACTV(PT[p][:, 0:N], STp[s][:, 0:N], AF.Exp, scale=0.125),
                         reads=[r_ST[s]], writes=[r_PT[p]])
                    d0 = tl["ifirst"] - j
                    if not tl["far"]:
                        emit(DVE, TT(PT[p][:, 0:N], PT[p][:, 0:N], cbf[:, WM0 + d0 * 128:WM0 + d0 * 128 + N], ALU.mult),
                             reads=[r_PT[p], r_const], writes=[r_PT[p]])
                else:
                    emit(ACT, ACTV(PT[p][:, 0:N], STp[s][:, 0:N], AF.Exp, scale=float(128.0 ** -0.5)),
                         reads=[r_ST[s]], writes=[r_PT[p]])

            def stage_C(tl):
                p = tl["p"]
                o = tl["o"]
                j = tl["j"]
                nb_ = tl["ilast"] - tl["ifirst"] + 1
                for bi, i in enumerate(range(tl["ifirst"], tl["ilast"] + 1)):
                    blk = i - tl["i0"]
                    lastmm = tl["last"] and (bi == nb_ - 1)
                    emit(PE, MM(OA[o][:, blk * W:blk * W + E + 1], PT[p][:, bi * 128:(bi + 1) * 128], v_of(j),
                                start=(tl["first"] and bi == 0), stop=lastmm, sgc=True),
                         reads=[r_PT[p], r_v], writes=[r_OA[o]])
                if tl["last"]:
                    emit(DVE, RC(rd[o][:, 0:nblk], AP(OA[o], E, [[512, 128], [W, nblk]])),
                         reads=[r_OA[o]], writes=[r_rd[o]])
                    for blk in range(nblk):
                        tt = tl["i0"] + blk
                        ga = G[:, tt * 2048 + gcol:tt * 2048 + gcol + E]
                        emit(DVE, STT(ga, OA[o][:, blk * W:blk * W + E], rd[o][:, blk:blk + 1], ga, ALU.mult, ALU.mult),
                             reads=[r_OA[o], r_rd[o], r_G[tt][grp]], writes=[r_G[tt][grp]])

            LA = 2 if kind == "mem" else 4
            for n in range(len(tiles) + LA):
                if n >= LA:
                    stage_C(tiles[n - LA])
                if n < len(tiles):
                    stage_A(tiles[n])
                    stage_B(tiles[n])
                if filler is not None and n % fill_every == fill_every - 1:
                    next(filler, None)

        def hT_ap(c, t0, n):
            return big[:, c * 2048 + t0:c * 2048 + t0 + n]

        def proj_pair(pp, slot):
            is_dil = pp >= 6
            wk = load_weights(wpair_d.ap()[pp], WSLOT)
            QA, QB, KA, KB, VP = Qa[slot], Qb[slot], Ka[slot], Kb[slot], Vp[slot]
            rQA, rQB, rKA, rKB, rVP = r_Qa[slot], r_Qb[slot], r_Ka[slot], r_Kb[slot], r_Vp[slot]
            if pp in (6, 7):
                emit(DVE, MS(QA[64:70, :], 0.0), writes=[rQA])
                emit(DVE, MS(KA[64:70, :], 0.0), writes=[rKA])
                emit(DVE, MS(QB[0:6, :], 0.0), writes=[rQB])
                emit(DVE, MS(KB[0:6, :], 0.0), writes=[rKB])
            units = []
            ropek = {}

            def mk_unit(tc, kind, u):
                cs = slice(tc * 512, (tc + 1) * 512)
                st_ = {}

                def P1():
                    if is_dil and kind == 0:
                        rk = nxt("rope", 2)
                        ropek[tc] = rk
                        emit(SP, DMA(AP(ropeb[rk], 0, [[1024, 128], [512, 2], [1, 512]]),
                                     AP(rope_d, tc * 512, [[4096, 128], [2048, 2], [1, 512]])),
                             writes=[r_rope[rk]], dsem=d_r[rk])
                    pk = nxt("pj", 2)
                    st_["pk"] = pk
                    for c in range(NCH):
                        emit(PE, MM(PJ[pk][:], wsl[wk][:, c * 384 + kind * 128:c * 384 + kind * 128 + 128],
                                    hT_ap(c, tc * 512, 512), start=(c == 0), stop=(c == NCH - 1)),
                             reads=[r_wsl[wk], r_hT[tc]], writes=[r_PJ[pk]])
                        if FINE and c % 2 == 1 and c < NCH - 1:
                            yield

                def P2():
                    pk = st_["pk"]
                    if kind == 2:
                        emit(ACT, ACTV(VT[:, cs], PJ[pk][:], AF.Copy), reads=[r_PJ[pk]], writes=[r_VT])
                    elif not is_dil:
                        if kind == 0:
                            emit(DVE, CP(QA[0:64, cs], PJ[pk][0:64, :]), reads=[r_PJ[pk]], writes=[rQA])
                            emit(DVE, CP(QB[64:128, cs], PJ[pk][64:128, :]), reads=[r_PJ[pk]], writes=[rQB])
                        else:
                            emit(DVE, CP(KA[0:64, cs], PJ[pk][0:64, :]), reads=[r_PJ[pk]], writes=[rKA])
                            emit(DVE, CP(KB[64:128, cs], PJ[pk][64:128, :]), reads=[r_PJ[pk]], writes=[rKB])
                    else:
                        a = u % 2
                        rk = ropek[tc]
                        emit(ACT, ACTV(tbs[a][:], PJ[pk][:], AF.Copy), reads=[r_PJ[pk]], writes=[r_tbs[a]])
                        emit(DVE, TT(tAs[a], PJ[pk][:], ropeb[rk][:, 0:512], ALU.mult),
                             reads=[r_PJ[pk], r_rope[rk]], writes=[r_tA[a], r_xs[0]])

                def P3():
                    if not is_dil or kind == 2:
                        return
                    a = u % 2
                    rk = ropek[tc]
                    tk = nxt("pj", 2)
                    emit(PE, MM(PJ[tk][:], pswap, tbs[a][:]), reads=[r_tbs[a], r_const], writes=[r_PJ[tk]])
                    emit(DVE, TT(tU, PJ[tk][:], ropeb[rk][:, 512:1024], ALU.mult),
                         reads=[r_PJ[tk], r_rope[rk]], writes=[r_tU, r_xt[0]])
                    tA0, tA1 = xs[0][0:64, a * 512:(a + 1) * 512], xs[0][64:128, a * 512:(a + 1) * 512]
                    tU0, tU1 = xt[0][0:64, 512:1024], xt[0][64:128, 512:1024]
                    if kind == 0:
                        emit(POOL, TT(QA[0:64, cs], tA0, tU0, ALU.add), reads=[r_tA[a], r_tU], writes=[rQA])
                        emit(POOL, TT(QB[64:128, cs], tA1, tU1, ALU.add), reads=[r_tA[a], r_tU], writes=[rQB])
                    else:
                        emit(POOL, TT(KA[0:64, cs], tA0, tU0, ALU.add), reads=[r_tA[a], r_tU], writes=[rKA])
                        emit(POOL, TT(KB[64:128, cs], tA1, tU1, ALU.add), reads=[r_tA[a], r_tU], writes=[rKB])

                return (P1, P2, P3)

            u = 0
            for tc in range(4):
                for kind in range(3):
                    units.append(mk_unit(tc, kind, u))
                    u += 1

            def mk_vt(h8):
                def P1():
                    tk = nxt("pj", 2)
                    for q in range(8):
                        tt = h8 * 8 + q
                        emit(PE, TP(PJb[tk][:, q * 128:(q + 1) * 128], VT[:, tt * 128:(tt + 1) * 128], ident),
                             reads=[r_VT, r_const], writes=[r_PJ[tk]])
                    emit(DVE, CP(AP(VP, h8 * 8 * 2 * VPW, [[16 * 2 * VPW, 128], [2 * VPW, 8], [VPW, 2], [1, 64]]),
                                 AP(PJb[tk], 0, [[1024, 128], [128, 8], [64, 2], [1, 64]])),
                         reads=[r_PJ[tk]], writes=[rVP])
                return (P1, None, None)

            def mk_aug(hl, which, h8):
                hidx = pp * 2 + hl
                qo, ko = (70, 73) if hl == 0 else (0, 3)
                c0, ncol, p0 = (6, 70, 64) if hl == 0 else (0, 6, 0)
                if which == 0:
                    X, dst, rdst = XQ, (QA if hl == 0 else QB), (rQA if hl == 0 else rQB)
                else:
                    X, dst, rdst = XK, (KA if hl == 0 else KB), (rKA if hl == 0 else rKB)

                def P1():
                    if h8 == 0:
                        if which == 0:
                            emit(DVE, CP(AP(XQ, qo, [[16 * XW, 128], [XW, 16], [1, 3]]),
                                         AP(P3, hidx * 3, [[576, 128], [36, 16], [1, 3]])),
                                 reads=[r_P3], writes=[r_X])
                        else:
                            emit(DVE, TS(AP(XK, ko, [[16 * XW, 128], [XW, 16], [1, 3]]),
                                         AP(P3, hidx * 3, [[576, 128], [36, 16], [1, 3]]), -1.0, ALU.mult),
                                 reads=[r_P3], writes=[r_X])
                    tk = nxt("pj", 2)
                    for q in range(8):
                        tt = h8 * 8 + q
                        emit(PE, TP(PJb[tk][0:ncol, q * 128:(q + 1) * 128],
                                    X[:, tt * XW + c0:tt * XW + c0 + ncol], ident),
                             reads=[r_X, r_const], writes=[r_PJ[tk]])
                    emit(DVE, CP(dst[p0:p0 + 6, h8 * 1024:(h8 + 1) * 1024], PJb[tk][p0:p0 + 6, 0:1024]),
                         reads=[r_PJ[tk]], writes=[rdst])
                return (P1, None, None)

            tail = [mk_vt(0), mk_vt(1)]
            if not is_dil:
                for hl in range(2):
                    for which in range(2):
                        for h8 in range(2):
                            tail.append(mk_aug(hl, which, h8))
            nu = len(units)
            for k in range(nu + 2):
                if 0 <= k - 2 < nu:
                    units[k - 2][2]()
                if 0 <= k - 1 < nu:
                    units[k - 1][1]()
                if k < nu:
                    for _ in units[k][0]():
                        yield
                yield
            for t_ in tail:
                t_[0]()
                yield

        for b in range(NB):
            mhT = yT
            jobs = [(x_d.ap()[b, tt * 128:(tt + 1) * 128, :], (g1T, big, 16384, 2048, tt * 128, r_hT[tt // 4])) for tt in range(NT)]
            jobs += [(mem_d.ap()[b, mt * 128:(mt + 1) * 128, :], (gmT, mhT, 2048, 256, mt * 128, r_yT)) for mt in range(2)]
            kprev = norm_part1(jobs[0][0])
            for ji in range(len(jobs)):
                knext = norm_part1(jobs[ji + 1][0]) if ji + 1 < len(jobs) else None
                norm_part2(kprev, *jobs[ji][1])
                kprev = knext
            def stage_c1():
                pk = nxt("pj", 2)
                for tt in range(NT):
                    for c in range(NCH):
                        emit(PE, MM(PJ[pk][:, tt * 12:(tt + 1) * 12], hT_ap(c, tt * 128, 128), wfl[:, c * 12:(c + 1) * 12],
                                    start=(c == 0), stop=(c == NCH - 1)),
                             reads=[r_const, r_hT[tt // 4]], writes=[r_PJ[pk]])
                emit(DVE, TT(fl, PJ[pk][:, 0:192], bfor[:], ALU.add), reads=[r_PJ[pk], r_const], writes=[r_sc] + r_tA)
                emit(ACT, ACTV(Lg, fl, AF.Exp, scale=-1.0), reads=[r_sc], writes=[r_sc])
                emit(ACT, ACTV(Lg, Lg, AF.Ln, bias=oneb[:, 0:1], scale=1.0), reads=[r_sc, r_zer], writes=[r_sc])

            def stage_c2():
                pk1 = nxt("pj", 2)
                emit(PE, MM(PJ[pk1][:, 0:192], negtri, Lg), reads=[r_sc, r_const], writes=[r_PJ[pk1]])
                pk2 = nxt("pj", 2)
                emit(PE, MM(PJ[pk2][:, 0:192], negones, Lg), reads=[r_sc, r_const], writes=[r_PJ[pk2]])
                emit(DVE, CP(tsb, PJ[pk2][:, 0:192]), reads=[r_PJ[pk2]], writes=[r_sc])
                emit(DVE, MS(pre[:, 0:12], 0.0), writes=[r_sc])
                for tt in range(1, NT):
                    emit(DVE, TT(pre[:, tt * 12:(tt + 1) * 12], pre[:, (tt - 1) * 12:tt * 12], tsb[:, (tt - 1) * 12:tt * 12], ALU.add),
                         reads=[r_sc], writes=[r_sc])
                emit(DVE, TT(ctm, PJ[pk1][:, 0:192], pre, ALU.add), reads=[r_PJ[pk1], r_sc], writes=[r_sc])

            def stage_c3():
                p3 = lambda t, k: AP(t, k, [[576, 128], [3, 192]])
                emit(DVE, TS(p3(P3, 0), ctm, 8.0, ALU.mult), reads=[r_sc], writes=[r_P3])
                emit(DVE, STT(fl, ctm, 8.0, p3(P3, 0), ALU.mult, ALU.subtract), reads=[r_sc, r_P3], writes=[r_sc])
                emit(DVE, CP(p3(P3, 1), fl), reads=[r_sc], writes=[r_P3])
                emit(DVE, TT(Lg, fl, p3(P3, 1), ALU.subtract), reads=[r_sc, r_P3], writes=[r_sc])
                emit(DVE, CP(p3(P3, 2), Lg), reads=[r_sc], writes=[r_P3])


            stage_c1()
            gen0 = None
            for g8 in range(8):
                wk = load_weights(wg8_d.ap()[g8], 2048)
                for t2 in range(NT // 2):
                    pk = nxt("pj", 2)
                    for hf in range(2):
                        tt = 2 * t2 + hf
                        for c in range(NCH):
                            emit(PE, MM(PJ[pk][:, hf * 256:(hf + 1) * 256], hT_ap(c, tt * 128, 128),
                                        wsl[wk][:, c * 256:(c + 1) * 256], start=(c == 0), stop=(c == NCH - 1)),
                                 reads=[r_wsl[wk], r_hT[tt // 4]], writes=[r_PJ[pk]])
                    emit(ACT, ACTV(AP(G, 2 * t2 * 2048 + g8 * 256, [[16 * 2048, 128], [2048, 2], [1, 256]]),
                                   AP(PJ[pk], 0, [[512, 128], [256, 2], [1, 256]]), AF.Silu),
                         reads=[r_PJ[pk]],
                         writes=[r_G[2 * t2][2 * g8], r_G[2 * t2][2 * g8 + 1], r_G[2 * t2 + 1][2 * g8], r_G[2 * t2 + 1][2 * g8 + 1]])
                    if gen0 is not None:
                        for _ in range(4 if FINE else 1):
                            next(gen0, None)
                if g8 == 0:
                    stage_c2()
                elif g8 == 1:
                    stage_c3()
                elif g8 == 5:
                    gen0 = proj_pair(0, 0)

            for _ in gen0:
                pass
            MQ = [Qa[0], Qb[0], Ka[0], Kb[0]]
            r_MQ = [r_Qa[0], r_Qb[0], r_Ka[0], r_Kb[0]]

            def proj_memkv():
                for k4 in range(4):
                    wk = load_weights(wmkv4_d.ap()[k4], 2048)
                    if k4 < 2:
                        for hl in range(2):
                            hh = 2 * k4 + hl
                            pk = nxt("pj", 2)
                            for c in range(NCH):
                                emit(PE, MM(PJ[pk][:, 0:256], wsl[wk][:, c * 256 + hl * 128:c * 256 + hl * 128 + 128],
                                            mhT[:, c * 256:(c + 1) * 256], start=(c == 0), stop=(c == NCH - 1)),
                                     reads=[r_wsl[wk], r_yT], writes=[r_PJ[pk]])
                            emit(DVE, CP(MK[:, hh * 256:(hh + 1) * 256], PJ[pk][:, 0:256]),
                                 reads=[r_PJ[pk]], writes=[r_MK])
                            yield
                    else:
                        h0 = 2 * (k4 - 2)
                        for mt in range(2):
                            pk = nxt("pj", 2)
                            for c in range(NCH):
                                emit(PE, MM(PJ[pk][:, 0:256], mhT[:, c * 256 + mt * 128:c * 256 + mt * 128 + 128],
                                            wsl[wk][:, c * 256:(c + 1) * 256], start=(c == 0), stop=(c == NCH - 1)),
                                     reads=[r_wsl[wk], r_yT], writes=[r_PJ[pk]])
                            emit(DVE, CP(AP(MVp, (mt * 4 + h0) * MVW, [[2 * 4 * MVW, 128], [MVW, 2], [1, 128]]),
                                         AP(PJ[pk], 0, [[512, 128], [128, 2], [1, 128]])),
                                 reads=[r_PJ[pk]], writes=[r_MVp])
                            yield

            def proj_mem():
                pend = None
                for q2 in range(2):
                    wk = load_weights(wmq2_d.ap()[q2], 2048)
                    for hl in range(2):
                        hh = q2 * 2 + hl
                        for tc in range(4):
                            pk = nxt("pj", 2)
                            for c in range(NCH):
                                emit(PE, MM(PJ[pk][:], wsl[wk][:, c * 256 + hl * 128:c * 256 + hl * 128 + 128],
                                            hT_ap(c, tc * 512, 512), start=(c == 0), stop=(c == NCH - 1)),
                                     reads=[r_wsl[wk], r_hT[tc]], writes=[r_PJ[pk]])
                                if FINE and c % 2 == 1 and c < NCH - 1:
                                    yield
                            if pend is not None:
                                pend()
                            if tc % 2 == 0:
                                pend = (lambda hh, tc, pk: lambda: emit(
                                    DVE, CP(MQ[hh][:, tc * 512:(tc + 1) * 512], PJ[pk][:]), reads=[r_PJ[pk]], writes=[r_MQ[hh]]))(hh, tc, pk)
                            else:
                                pend = (lambda hh, tc, pk: lambda: emit(
                                    ACT, ACTV(MQ[hh][:, tc * 512:(tc + 1) * 512], PJ[pk][:], AF.Copy), reads=[r_PJ[pk]], writes=[r_MQ[hh]]))(hh, tc, pk)
                            yield
                pend()
                yield

            for pp in range(12):
                slot = pp % 2
                is_dil = pp >= 6
                nxt_gen = proj_pair(pp + 1, (pp + 1) % 2) if pp + 1 < 12 else chain_gens(proj_memkv(), proj_mem())
                for hl in range(2):
                    hidx = (pp % 6) * 2 + hl
                    qT = Qa[slot] if hl == 0 else Qb[slot]
                    kT = Ka[slot] if hl == 0 else Kb[slot]
                    r_q = r_Qa[slot] if hl == 0 else r_Qb[slot]
                    r_k = r_Ka[slot] if hl == 0 else r_Kb[slot]
                    v_of = (lambda hl, VP: lambda j: VP[:, (j * 2 + hl) * VPW:(j * 2 + hl) * VPW + 65])(hl, Vp[slot])
                    if not is_dil:
                        attention("fox", qT, kT, 0, v_of, 64, 4, hidx * 64, pp, r_q, r_k, r_Vp[slot], filler=nxt_gen)
                    else:
                        attention("dil", qT, kT, 0, v_of, 64, 4, 768 + hidx * 64, pp, r_q, r_k, r_Vp[slot], filler=nxt_gen)
                if nxt_gen is not None:
                    for _ in nxt_gen:
                        pass

            for q4 in range(4):
                emit(POOL, DMA(big[:, q4 * 4096:(q4 + 1) * 4096], wout_d.ap()[:, q4 * 4096:(q4 + 1) * 4096]),
                     writes=([r_wout] + r_hT) if q4 == 0 else [], dsem=d_wo)
            for r_ in [r_wout] + r_hT:
                r_.w[d_wo] = d_wo.n
            for hh in range(4):
                v_of = (lambda hh: lambda j: MVp[:, (j * 4 + hh) * MVW:(j * 4 + hh) * MVW + 129])(hh)
                attention("mem", MQ[hh], MK, hh * 256, v_of, 128, 2, 1536 + hh * 128, 12 + hh, r_MQ[hh], r_MK, r_MVp)
            if b + 1 < NB:
                for hh in range(4):
                    emit(POOL, MS(MQ[hh][:], 0.0), writes=[r_MQ[hh]])

            yTb = [yT, Qa[1]]
            r_yTb = [r_yT, r_Qa[1]]

            FT = [STp[0].bitcast(BF16), STp[1].bitcast(BF16), OA[0].bitcast(BF16), OA[1].bitcast(BF16)]
            r_FT = [r_ST[0], r_ST[1], r_OA[0], r_OA[1]]
            FM = [PJ[0], PJ[1], STp[2], STp[3]]
            r_FM = [r_PJ[0], r_PJ[1], r_ST[2], r_ST[3]]
            fcnt = dict(t=0, m=0)

            def f_transposes(tt):
                yk = tt % 2
                for h8 in range(2):
                    tk = fcnt["t"] % 4
                    fcnt["t"] += 1
                    for q in range(8):
                        c = h8 * 8 + q
                        emit(PE, TP(FT[tk][:, q * 128:(q + 1) * 128], G[:, tt * 2048 + c * 128:tt * 2048 + (c + 1) * 128], ident),
                             reads=[r_G[tt][c], r_const], writes=[r_FT[tk]])
                    emit(DVE, CP(yTb[yk][:, h8 * 1024:(h8 + 1) * 1024], FT[tk][:, 0:1024]),
                         reads=[r_FT[tk]], writes=[r_yTb[yk]])

            f_transposes(0)
            for tt in range(NT):
                yk = tt % 2
                k = nxt("x", 2)
                emit(SP, DMA(xt[k][:], x_d.ap()[b, tt * 128:(tt + 1) * 128, :]), writes=[r_xt[k]] + ([r_tU] if k == 0 else []), dsem=d_x[k])
                if tt + 1 < NT:
                    f_transposes(tt + 1)
                pks = []
                for half in range(2):
                    pk = fcnt["m"] % 4
                    fcnt["m"] += 1
                    pks.append(pk)
                    for c in range(16):
                        emit(PE, MM(FM[pk][:], yTb[yk][:, c * 128:(c + 1) * 128],
                                    big[:, c * 1024 + half * 512:c * 1024 + (half + 1) * 512],
                                    start=(c == 0), stop=(c == 15)),
                             reads=[r_yTb[yk], r_wout] + r_hT, writes=[r_FM[pk]])
                for half in range(2):
                    pk = pks[half]
                    emit(DVE, TT(xt[k][:, half * 512:(half + 1) * 512], FM[pk][:], xt[k][:, half * 512:(half + 1) * 512], ALU.add),
                         reads=[r_FM[pk], r_xt[k]], writes=[r_xt[k]])
                rms_stats(xt[k][:], k, r_xt[k])
                emit(DVE, STT(xs[k][:], xt[k][:], st[k][:, 2:3], gf[:], ALU.mult, ALU.mult),
                     reads=[r_xt[k], r_st[k], r_const], writes=[r_xs[k]])
                emit(SP, DMA(out_d.ap()[b, tt * 128:(tt + 1) * 128, :], xs[k][:]), reads=[r_xs[k]], dsem=d_o[k])
            if b + 1 < NB:
                emit(POOL, MS(Qa[1][:], 0.0), writes=[r_Qa[1]])

        for k in range(2):
            wait_tok(SP, d_o[k], d_o[k].n)

        for e in engines:
            e.sem = es.enter_context(nc.semaphore("sem_" + e.name))
        for d in dsems:
            d.sem = es.enter_context(nc.semaphore("dsem_" + d.name))
        with nc.Block() as block:
            @block.tensor
            def _(t):
                replay(PE, t)

            @block.scalar
            def _(a):
                replay(ACT, a)

            @block.vector
            def _(v):
                replay(DVE, v)

            @block.gpsimd
            def _(g):
                replay(POOL, g)

            @block.sync
            def _(s):
                replay(SP, s)
    return nc


def _chunked(w, ncols):
    return np.ascontiguousarray(w.reshape(8, 128, ncols).transpose(1, 0, 2).reshape(128, 8 * ncols))


def _constants():
    p = np.arange(128)
    ident = np.eye(128, dtype=np.float32)
    negmask = np.where(p[:, None] <= p[None, :], 0.0, -30000.0).astype(np.float32)
    pswap = np.zeros((128, 128), np.float32)
    for m in range(128):
        ml = m % 64
        if ml < 8:
            pswap[m + 8, m] = -1.0
        elif ml < 16:
            pswap[m - 8, m] = 1.0
    wmask = np.zeros((128, 16, 128), np.float32)
    for dlt in range(16):
        delta = 128 * dlt + p[None, :] - p[:, None]
        m1 = (delta >= 0) & (delta <= 128)
        m2 = (delta >= 0) & (delta % 4 == 0) & (delta <= 512)
        m3 = (delta >= 0) & (delta % 16 == 0) & (delta <= 2048)
        wmask[:, dlt, :] = m1.astype(np.float32) + m2 + m3
    farmask = np.where((p[None, :] - p[:, None]) % 16 == 0, 0.0, -30000.0).astype(np.float32)
    cbf = np.concatenate([ident, negmask, pswap, wmask.reshape(128, 2048), farmask], axis=1)
    negtri = -(p[:, None] <= p[None, :]).astype(np.float32)
    negones = -np.ones((128, 128), np.float32)
    cf = np.concatenate([ident, negtri, negones], axis=1)
    pos = np.arange(S, dtype=np.float32)
    inv_freq = (1.0 / (np.float32(500000.0) ** (np.arange(0, 16, 2, dtype=np.float32) / np.float32(16)))).astype(np.float32)
    ang = (pos[:, None] * inv_freq[None, :]).astype(np.float32)
    C = np.ones((128, S), np.float32)
    Sn = np.zeros((128, S), np.float32)
    for m in range(128):
        ml = m % 64
        if ml < 16:
            C[m] = np.cos(ang[:, ml % 8])
            Sn[m] = np.sin(ang[:, ml % 8])
    rope = np.concatenate([C, Sn], axis=1).astype(np.float32)
    return np.ascontiguousarray(cbf), np.ascontiguousarray(cf), np.ascontiguousarray(rope)


_PROGRAM = None


def kernel(x, mem, norm_g, w_in, b_forget, mem_norm_g, w_mem_kv, w_out, final_norm_g):
    global _PROGRAM
    x = np.asarray(x, dtype=np.float32)
    mem = np.asarray(mem, dtype=np.float32)
    w = np.asarray(w_in, dtype=np.float32)[0]
    sizes = [768] * 4 + [12] + [768] * 4 + [512] * 2
    offs = np.cumsum([0] + sizes)
    fq, fk, fv, fg, flg, dq, dk, dv, dg, mq, mg = [w[:, offs[i]:offs[i + 1]] for i in range(11)]
    wpair = np.zeros((12, 128, WSLOT), np.float32)
    for pp in range(6):
        cs = slice(pp * 128, (pp + 1) * 128)
        wpair[pp] = _chunked(np.concatenate([fq[:, cs], fk[:, cs], fv[:, cs]], axis=1), 384)
        wpair[6 + pp] = _chunked(np.concatenate([dq[:, cs], dk[:, cs], dv[:, cs]], axis=1), 384)
    gates = np.concatenate([fg, dg, mg], axis=1)
    wg8 = np.stack([_chunked(gates[:, i * 256:(i + 1) * 256], 256) for i in range(8)])
    wmq2 = np.stack([_chunked(mq[:, i * 256:(i + 1) * 256], 256) for i in range(2)])
    wkv = np.asarray(w_mem_kv, dtype=np.float32)[0]
    wmkv4 = np.stack([_chunked(wkv[:, i * 256:(i + 1) * 256], 256) for i in range(4)])
    wfl = _chunked(flg, 12)
    wo = np.asarray(w_out, dtype=np.float32)[0]
    wout = np.ascontiguousarray(wo.reshape(16, 128, 1024).transpose(1, 0, 2).reshape(128, 16384))
    gfb = np.ascontiguousarray(np.broadcast_to(np.asarray(final_norm_g, np.float32)[None, :], (128, 1024)))
    g1T = np.ascontiguousarray(np.asarray(norm_g, np.float32)[0].reshape(8, 128).T)
    gmT = np.ascontiguousarray(np.asarray(mem_norm_g, np.float32)[0].reshape(8, 128).T)
    bfor = np.ascontiguousarray(np.broadcast_to(np.asarray(b_forget, np.float32)[0][None, None, :], (128, 16, 12)).reshape(128, 192))
    cbf, cf, rope = _constants()

    if _PROGRAM is None:
        _PROGRAM = build_program()
    nc = _PROGRAM
    shared = dict(wpair=wpair, wg8=wg8, wmq2=wmq2, wmkv4=wmkv4, wfl=wfl, wout=wout, gf=gfb, g1T=g1T, gmT=gmT,
                  bfor=bfor, cbf=cbf, cf=cf, rope=rope)
    in_maps = []
    for c in range(NCORES):
        m = dict(shared)
        m["x"] = np.ascontiguousarray(x[c * NB:(c + 1) * NB])
        m["mem"] = np.ascontiguousarray(mem[c * NB:(c + 1) * NB])
        in_maps.append(m)
    res = run_bass_kernel_spmd(nc, in_maps, core_ids=list(range(NCORES)))
    out = np.concatenate([np.asarray(r["out"], dtype=np.float32) for r in res.results], axis=0)
    return out
```

```python
import bisect
from contextlib import ExitStack

import numpy as np
import concourse.bass as bass
import concourse.mybir as mybir
from concourse.bass_utils import run_bass_kernel_spmd

F32 = mybir.dt.float32
BF16 = mybir.dt.bfloat16
AF = mybir.ActivationFunctionType
ALU = mybir.AluOpType

NCORES = 8
NB = 2
S = 2048
D = 1024
NT = 16
NCH = 8
MEM = 256
EPS = 1e-6
WSLOT = 3072
FINE = False
FILL_EVERY = 4


class Engine:
    def __init__(self, name, skip_self=False):
        self.name = name
        self.ops = []
        self.n = 0
        self.waited = {}
        self.refd = set()
        self.skip_self = skip_self
        self.sem = None
        self._sorted = None

    def resolve(self, i):
        if self._sorted is None:
            self._sorted = sorted(self.refd)
        return bisect.bisect_right(self._sorted, i)


class DSem:
    def __init__(self, name):
        self.name = name
        self.n = 0
        self.sem = None

    def resolve(self, i):
        return 16 * i


class Res:
    def __init__(self, name="", excl=False):
        self.name = name
        self.w = {}
        self.r = {}
        self.excl = excl


def emit(eng, fn, reads=(), writes=(), dsem=None):
    need = {}
    for r in reads:
        for o, i in r.w.items():
            if need.get(o, 0) < i:
                need[o] = i
        if r.excl:
            for o, i in r.r.items():
                if o is not eng and need.get(o, 0) < i:
                    need[o] = i
    for w in writes:
        for o, i in w.w.items():
            if need.get(o, 0) < i:
                need[o] = i
        for o, i in w.r.items():
            if need.get(o, 0) < i:
                need[o] = i
    for o, i in need.items():
        if o is eng and eng.skip_self:
            continue
        if eng.waited.get(o, 0) >= i:
            continue
        eng.waited[o] = i
        if isinstance(o, Engine):
            o.refd.add(i)
        eng.ops.append(("wait", o, i))
    if dsem is None:
        eng.n += 1
        tok = (eng, eng.n)
        eng.ops.append(("op", fn, eng.n))
    else:
        dsem.n += 1
        tok = (dsem, dsem.n)
        eng.ops.append(("dma", fn, dsem))
    for r in reads:
        if r.r.get(tok[0], 0) < tok[1]:
            r.r[tok[0]] = tok[1]
    for w in writes:
        w.w[tok[0]] = tok[1]
        w.r = {}
    return tok


def wait_tok(eng, obj, idx):
    if eng.waited.get(obj, 0) >= idx:
        return
    eng.waited[obj] = idx
    if isinstance(obj, Engine):
        obj.refd.add(idx)
    eng.ops.append(("wait", obj, idx))


def replay(eng, h):
    for op in eng.ops:
        if op[0] == "wait":
            h.wait_ge(op[1].sem, op[1].resolve(op[2]))
        elif op[0] == "op":
            inst = op[1](h)
            if op[2] in eng.refd:
                inst.then_inc(eng.sem, 1)
        else:
            inst = op[1](h)
            inst.then_inc(op[2].sem, 16)


def chain_gens(*gens):
    for g in gens:
        for _ in g:
            yield


def MM(out, lhsT, rhs, start=True, stop=True, sgc=False):
    return lambda t: t.matmul(out, lhsT=lhsT, rhs=rhs, start=start, stop=stop, skip_group_check=sgc)


def TP(out, in_, idn):
    return lambda t: t.transpose(out, in_, idn)


def ACTV(out, in_, func, bias=0.0, scale=1.0, accum_out=None):
    if accum_out is None:
        return lambda a: a.activation(out=out, in_=in_, func=func, bias=bias, scale=scale)
    return lambda a: a.activation(out=out, in_=in_, func=func, bias=bias, scale=scale, accum_out=accum_out)


def TT(out, in0, in1, op):
    return lambda v: v.tensor_tensor(out=out, in0=in0, in1=in1, op=op)


def TS(out, in0, s1, op0):
    return lambda v: v.tensor_scalar(out=out, in0=in0, scalar1=s1, scalar2=None, op0=op0)


def STT(out, in0, scalar, in1, op0, op1):
    return lambda v: v.scalar_tensor_tensor(out=out, in0=in0, scalar=scalar, in1=in1, op0=op0, op1=op1)


def CP(out, in_):
    return lambda v: v.tensor_copy(out=out, in_=in_)


def MS(ap, val):
    return lambda v: v.memset(ap, val)


def RC(out, in_):
    return lambda v: v.reciprocal(out=out, in_=in_)


def DMA(out, in_):
    return lambda q: q.dma_start(out=out, in_=in_)

def build_program():
    nc = bass.Bass("TRN2", target_bir_lowering=False)

    def din(name, shape):
        return nc.dram_tensor(name, list(shape), F32, kind="ExternalInput")

    x_d = din("x", [NB, S, D])
    mem_d = din("mem", [NB, MEM, D])
    wpair_d = din("wpair", [12, 128, WSLOT])
    wg8_d = din("wg8", [8, 128, 2048])
    wmq2_d = din("wmq2", [2, 128, 2048])
    wmkv4_d = din("wmkv4", [4, 128, 2048])
    wfl_d = din("wfl", [128, 96])
    wout_d = din("wout", [128, 16384])
    gf_d = din("gf", [128, 1024])
    g1T_d = din("g1T", [128, 8])
    gmT_d = din("gmT", [128, 8])
    bfor_d = din("bfor", [128, 192])
    cbf_d = din("cbf", [128, 128 * 4 + 2048])
    cf_d = din("cf", [128, 128 * 3])
    rope_d = din("rope", [128, 4096])
    out_d = nc.dram_tensor("out", [NB, S, D], F32, kind="ExternalOutput")

    PE = Engine("pe", skip_self=True)
    ACT = Engine("act")
    DVE = Engine("dve")
    POOL = Engine("pool")
    SP = Engine("sp")
    engines = [PE, ACT, DVE, POOL, SP]
    dsems = []

    def mkdsem(name):
        d = DSem(name)
        dsems.append(d)
        return d

    es = ExitStack()
    with es:
        def sb(name, free, dt):
            return es.enter_context(nc.sbuf_tensor("s_" + name, [128, free], dt))

        def ps(name, free, dt):
            return es.enter_context(nc.psum_tensor("p_" + name, [128, free], dt))

        big = sb("big", 16384, BF16)
        G = sb("G", 16 * 2048, BF16)
        Qa = [sb("Qa%d" % i, 2048, BF16) for i in range(2)]
        Qb = [sb("Qb%d" % i, 2048, BF16) for i in range(2)]
        Ka = [sb("Ka%d" % i, 2048, BF16) for i in range(2)]
        Kb = [sb("Kb%d" % i, 2048, BF16) for i in range(2)]
        VPW = 66
        Vp = [sb("Vp%d" % i, 16 * 2 * VPW, BF16) for i in range(2)]
        wsl = [sb("wsl0", WSLOT, BF16), sb("wsl1", WSLOT, BF16)]
        wfl = sb("wfl", 96, BF16)
        xt = [sb("xt0", 1024, F32), sb("xt1", 1024, F32)]
        xs = [sb("xs0", 1024, F32), sb("xs1", 1024, F32)]
        VT = xt[1].bitcast(BF16)
        tAs = [xs[0][:, 0:512], xs[0][:, 512:1024]]
        tU = xt[0][:, 512:1024]
        fl = xs[0][:, 0:192]
        Lg = xs[0][:, 192:384]
        tsb = xs[0][:, 384:576]
        pre = xs[0][:, 576:768]
        ctm = xs[0][:, 768:960]
        gf = sb("gf", 1024, F32)
        g1T = sb("g1T", 8, F32)
        gmT = sb("gmT", 8, F32)
        bfor = sb("bfor", 192, F32)
        PT = [sb("PT%d" % i, 512, BF16) for i in range(5)]
        cbf = sb("cbf", 128 * 4 + 2048, BF16)
        epsb = sb("epsb", 1, F32)
        oneb = sb("oneb", 1, F32)
        cf = sb("cf", 384, F32)
        ropeb = [sb("ropeb0", 1024, F32), sb("ropeb1", 1024, F32)]
        MK = sb("MK", 4 * 256, BF16)
        MVW = 130
        MVp = sb("MVp", 2 * 4 * MVW, BF16)
        tbs = [sb("tb0", 512, BF16), sb("tb1", 512, BF16)]
        yT = sb("yT", 2048, BF16)
        XW = 80
        NP = 4
        XQ = sb("XQ", 16 * XW, BF16)
        XK = sb("XK", 16 * XW, BF16)
        P3 = sb("P3", 192 * 4, BF16)
        st = [sb("st%d" % i, 8, F32) for i in range(2)]
        rd = [sb("rd%d" % i, 4, F32) for i in range(2)]

        PJ = [ps("PJ0", 512, F32), ps("PJ1", 512, F32)]
        STp = [ps("ST%d" % i, 512, F32) for i in range(4)]
        OA = [ps("OA0", 512, F32), ps("OA1", 512, F32)]
        PJb = [t.bitcast(BF16) for t in PJ]

        R = lambda n: Res(n)
        r_hT = [R("hT%d" % i) for i in range(4)]
        r_wout = R("wout")
        r_G = [[R("G") for _ in range(16)] for _ in range(16)]
        r_Qa = [R("Qa0"), R("Qa1")]
        r_Qb = [R("Qb0"), R("Qb1")]
        r_Ka = [R("Ka0"), R("Ka1")]
        r_Kb = [R("Kb0"), R("Kb1")]
        r_Vp = [R("Vp0"), R("Vp1")]
        r_wsl = [R("wsl0"), R("wsl1")]
        r_xt = [R("xt0"), R("xt1")]
        r_xs = [R("xs0"), R("xs1")]
        r_VT = r_xt[1]
        r_sc = r_xs[0]
        r_const = R("const")
        r_zer = R("zer")
        r_PT = [R("PT") for _ in range(5)]
        r_rope = [R("rope0"), R("rope1")]
        r_MK, r_MVp = R("MK"), R("MVp")
        r_tbs = [R("tb0"), R("tb1")]
        r_tA = [R("tA0"), R("tA1")]
        r_tU = R("tU")
        r_yT = R("yT")
        r_X = R("X")
        r_P3 = R("P3")
        r_st = [R("st0"), R("st1")]
        r_rd = [R("rd0"), R("rd1")]
        r_PJ = [Res("PJ0", True), Res("PJ1", True)]
        r_ST = [Res("ST%d" % i, True) for i in range(4)]
        r_OA = [Res("OA0", True), Res("OA1", True)]

        d_x = [mkdsem("dx0"), mkdsem("dx1")]
        d_w = [mkdsem("dw0"), mkdsem("dw1")]
        d_r = [mkdsem("dr0"), mkdsem("dr1")]
        d_o = [mkdsem("do0"), mkdsem("do1")]
        d_wo = mkdsem("dwo")

        def AP(t, off, dims):
            return bass.AP(t, off, [list(d) for d in dims])

        ident = cbf[:, 0:128]
        negmask = cbf[:, 128:256]
        pswap = cbf[:, 256:384]
        WM0 = 384
        farmask = cbf[:, 384 + 2048:384 + 2048 + 128]
        identf = cf[:, 0:128]
        negtri = cf[:, 128:256]
        negones = cf[:, 256:384]

        def load_const(q, dst_ap, src_ap, name):
            d = mkdsem("dc_" + name)
            emit(q, DMA(dst_ap, src_ap), dsem=d)
            r_const.w[d] = 1

        load_const(POOL, cbf[:], cbf_d.ap(), "cbf")
        load_const(POOL, wfl[:], wfl_d.ap(), "wfl")
        load_const(SP, cf[:], cf_d.ap(), "cf")
        load_const(SP, gf[:], gf_d.ap(), "gf")
        load_const(SP, g1T[:], g1T_d.ap(), "g1T")
        load_const(SP, gmT[:], gmT_d.ap(), "gmT")
        load_const(SP, bfor[:], bfor_d.ap(), "bfor")
        emit(DVE, MS(epsb[:], EPS), writes=[r_zer])
        emit(DVE, MS(oneb[:], 1.0), writes=[r_zer])
        for i in range(2):
            emit(DVE, MS(Qa[i][:], 0.0), writes=[r_Qa[i]])
            emit(DVE, MS(Qb[i][:], 0.0), writes=[r_Qb[i]])
            emit(DVE, MS(Ka[i][:], 0.0), writes=[r_Ka[i]])
            emit(DVE, MS(Kb[i][:], 0.0), writes=[r_Kb[i]])
            emit(DVE, MS(AP(Vp[i], 64, [[16 * 2 * VPW, 128], [VPW, 32], [1, 1]]), 1.0), writes=[r_Vp[i]])
        emit(DVE, MS(AP(MVp, 128, [[2 * 4 * MVW, 128], [MVW, 8], [1, 1]]), 1.0), writes=[r_MVp])
        emit(DVE, MS(XQ[:], 0.0), writes=[r_X])
        emit(DVE, MS(XK[:], 0.0), writes=[r_X])
        for off in (4, 76):
            emit(DVE, MS(AP(XQ, off, [[16 * XW, 128], [XW, 16], [1, 4]]), 1.0), writes=[r_X])
        for off in (0, 72):
            emit(DVE, MS(AP(XK, off, [[16 * XW, 128], [XW, 16], [1, 4]]), 1.0), writes=[r_X])

        cnt = dict(pj=0, x=0, w=0, st=0, pt=0, oa=0, rope=0)

        def nxt(key, mod):
            k = cnt[key] % mod
            cnt[key] += 1
            return k

        def load_weights(src_ap, ncols_total):
            k = nxt("w", 2)
            emit(POOL, DMA(wsl[k][:, 0:ncols_total], src_ap), writes=[r_wsl[k]], dsem=d_w[k])
            return k

        def rms_stats(src_ap, k, r_src):
            emit(ACT, ACTV(xs[k][:], src_ap, AF.Square, accum_out=st[k][:, 0:1]),
                 reads=[r_src], writes=[r_xs[k], r_st[k]] + (r_tA if k == 0 else []))
            emit(ACT, ACTV(st[k][:, 1:2], st[k][:, 0:1], AF.Ln, bias=epsb[:, 0:1], scale=1.0 / D),
                 reads=[r_st[k], r_zer], writes=[r_st[k]])
            emit(ACT, ACTV(st[k][:, 2:3], st[k][:, 1:2], AF.Exp, scale=-0.5),
                 reads=[r_st[k]], writes=[r_st[k]])

        def norm_part1(src_rows):
            k = nxt("x", 2)
            emit(SP, DMA(xt[k][:], src_rows), writes=[r_xt[k]] + ([r_tU] if k == 0 else []), dsem=d_x[k])
            rms_stats(xt[k][:], k, r_xt[k])
            emit(DVE, TS(xs[k][:], xt[k][:], st[k][:, 2:3], ALU.mult),
                 reads=[r_xt[k], r_st[k]], writes=[r_xs[k]])
            return k

        def norm_part2(k, gT, dst_t, dst_free, dst_stride, dst_off, r_dst):
            for half in range(2):
                tk = nxt("pj", 2)
                for q in range(4):
                    c = half * 4 + q
                    emit(PE, TP(PJ[tk][:, q * 128:(q + 1) * 128], xs[k][:, c * 128:(c + 1) * 128], identf),
                         reads=[r_xs[k], r_const], writes=[r_PJ[tk]])
                emit(DVE, TT(AP(dst_t, half * 4 * dst_stride + dst_off, [[dst_free, 128], [dst_stride, 4], [1, 128]]),
                             AP(PJ[tk], 0, [[512, 128], [128, 4], [1, 128]]),
                             AP(gT, half * 4, [[8, 128], [1, 4], [0, 128]]), ALU.mult),
                     reads=[r_PJ[tk], r_const], writes=[r_dst])

        def attention(kind, qT, kT, kbase, v_of, E, nblk, gcol, grp, r_q, r_k, r_v, filler=None, fill_every=FILL_EVERY):
            W = 512 // nblk
            nchunk = NT // nblk
            tiles = []
            for qc in range(nchunk):
                i0 = qc * nblk
                js = [0, 1] if kind == "mem" else list(range(i0 + nblk))
                for j in js:
                    ifirst = i0 if kind == "mem" else max(i0, j)
                    tiles.append(dict(qc=qc, j=j, ifirst=ifirst, ilast=i0 + nblk - 1, i0=i0,
                                      first=(j == js[0]), last=(j == js[-1])))
            state = {}

            def stage_A(tl):
                s = nxt("st", 4)
                tl["s"] = s
                if tl["first"]:
                    o = nxt("oa", 2)
                    state["o"] = o
                tl["o"] = state["o"]
                j = tl["j"]
                n0 = tl["ifirst"] * 128
                N = (tl["ilast"] - tl["ifirst"] + 1) * 128
                tl["N"] = N
                diag = (kind == "fox") and (tl["ifirst"] == j)
                far = (kind == "dil") and (tl["ifirst"] - j >= 5)
                tl["far"] = far
                emit(PE, MM(STp[s][:, 0:N], kT[:, kbase + j * 128:kbase + (j + 1) * 128], qT[:, n0:n0 + N],
                            start=True, stop=not (diag or far), sgc=True),
                     reads=[r_q, r_k], writes=[r_ST[s]])
                if far:
                    nb_ = N // 128
                    for bi in range(nb_):
                        emit(PE, MM(STp[s][:, bi * 128:(bi + 1) * 128], ident, farmask, start=False, stop=(bi == nb_ - 1), sgc=True),
                             reads=[r_const], writes=[r_ST[s]])
                if diag:
                    emit(PE, MM(STp[s][:, 0:128], ident, negmask, start=False, stop=True, sgc=True),
                         reads=[r_const], writes=[r_ST[s]])

            def stage_B(tl):
                s = tl["s"]
                p = nxt("pt", 5)
                tl["p"] = p
                N = tl["N"]
                j = tl["j"]
                if kind == "fox":
                    emit(ACT, ACTV(PT[p][:, 0:N], STp[s][:, 0:N], AF.Exp, scale=0.125),
                         reads=[r_ST[s]], writes=[r_PT[p]])
                elif kind == "dil":
                    emit(ACT, ACTV(PT[p][:, 0:N], STp[s][:, 0:N], AF.Exp, scale=0.125),
                         reads=[r_ST[s]], writes=[r_PT[p]])
                    d0 = tl["ifirst"] - j
                    if not tl["far"]:
                        emit(DVE, TT(PT[p][:, 0:N], PT[p][:, 0:N], cbf[:, WM0 + d0 * 128:WM0 + d0 * 128 + N], ALU.mult),
                             reads=[r_PT[p], r_const], writes=[r_PT[p]])
                else:
                    emit(ACT, ACTV(PT[p][:, 0:N], STp[s][:, 0:N], AF.Exp, scale=float(128.0 ** -0.5)),
                         reads=[r_ST[s]], writes=[r_PT[p]])

            def stage_C(tl):
                p = tl["p"]
                o = tl["o"]
                j = tl["j"]
                nb_ = tl["ilast"] - tl["ifirst"] + 1
                for bi, i in enumerate(range(tl["ifirst"], tl["ilast"] + 1)):
                    blk = i - tl["i0"]
                    lastmm = tl["last"] and (bi == nb_ - 1)
                    emit(PE, MM(OA[o][:, blk * W:blk * W + E + 1], PT[p][:, bi * 128:(bi + 1) * 128], v_of(j),
                                start=(tl["first"] and bi == 0), stop=lastmm, sgc=True),
                         reads=[r_PT[p], r_v], writes=[r_OA[o]])
                if tl["last"]:
                    emit(DVE, RC(rd[o][:, 0:nblk], AP(OA[o], E, [[512, 128], [W, nblk]])),
                         reads=[r_OA[o]], writes=[r_rd[o]])
                    for blk in range(nblk):
                        tt = tl["i0"] + blk
                        ga = G[:, tt * 2048 + gcol:tt * 2048 + gcol + E]
                        emit(DVE, STT(ga, OA[o][:, blk * W:blk * W + E], rd[o][:, blk:blk + 1], ga, ALU.mult, ALU.mult),
                             reads=[r_OA[o], r_rd[o], r_G[tt][grp]], writes=[r_G[tt][grp]])

            LA = 2 if kind == "mem" else 4
            for n in range(len(tiles) + LA):
                if n >= LA:
                    stage_C(tiles[n - LA])
                if n < len(tiles):
                    stage_A(tiles[n])
                    stage_B(tiles[n])
                if filler is not None and n % fill_every == fill_every - 1:
                    next(filler, None)

        def hT_ap(c, t0, n):
            return big[:, c * 2048 + t0:c * 2048 + t0 + n]

        def proj_pair(pp, slot):
            is_dil = pp >= 6
            wk = load_weights(wpair_d.ap()[pp], WSLOT)
            QA, QB, KA, KB, VP = Qa[slot], Qb[slot], Ka[slot], Kb[slot], Vp[slot]
            rQA, rQB, rKA, rKB, rVP = r_Qa[slot], r_Qb[slot], r_Ka[slot], r_Kb[slot], r_Vp[slot]
            if pp in (6, 7):
                emit(DVE, MS(QA[64:72, :], 0.0), writes=[rQA])
                emit(DVE, MS(KA[64:72, :], 0.0), writes=[rKA])
                emit(DVE, MS(QB[0:8, :], 0.0), writes=[rQB])
                emit(DVE, MS(KB[0:8, :], 0.0), writes=[rKB])
            units = []
            ropek = {}

            def mk_unit(tc, kind, u):
                cs = slice(tc * 512, (tc + 1) * 512)
                st_ = {}

                def P1():
                    if is_dil and kind == 0:
                        rk = nxt("rope", 2)
                        ropek[tc] = rk
                        emit(SP, DMA(AP(ropeb[rk], 0, [[1024, 128], [512, 2], [1, 512]]),
                                     AP(rope_d, tc * 512, [[4096, 128], [2048, 2], [1, 512]])),
                             writes=[r_rope[rk]], dsem=d_r[rk])
                    pk = nxt("pj", 2)
                    st_["pk"] = pk
                    for c in range(NCH):
                        emit(PE, MM(PJ[pk][:], wsl[wk][:, c * 384 + kind * 128:c * 384 + kind * 128 + 128],
                                    hT_ap(c, tc * 512, 512), start=(c == 0), stop=(c == NCH - 1)),
                             reads=[r_wsl[wk], r_hT[tc]], writes=[r_PJ[pk]])
                        if FINE and c % 2 == 1 and c < NCH - 1:
                            yield

                def P2():
                    pk = st_["pk"]
                    if kind == 2:
                        emit(ACT, ACTV(VT[:, cs], PJ[pk][:], AF.Copy), reads=[r_PJ[pk]], writes=[r_VT])
                    elif not is_dil:
                        if kind == 0:
                            emit(DVE, CP(QA[0:64, cs], PJ[pk][0:64, :]), reads=[r_PJ[pk]], writes=[rQA])
                            emit(DVE, CP(QB[64:128, cs], PJ[pk][64:128, :]), reads=[r_PJ[pk]], writes=[rQB])
                        else:
                            emit(DVE, CP(KA[0:64, cs], PJ[pk][0:64, :]), reads=[r_PJ[pk]], writes=[rKA])
                            emit(DVE, CP(KB[64:128, cs], PJ[pk][64:128, :]), reads=[r_PJ[pk]], writes=[rKB])
                    else:
                        a = u % 2
                        rk = ropek[tc]
                        emit(ACT, ACTV(tbs[a][:], PJ[pk][:], AF.Copy), reads=[r_PJ[pk]], writes=[r_tbs[a]])
                        emit(DVE, TT(tAs[a], PJ[pk][:], ropeb[rk][:, 0:512], ALU.mult),
                             reads=[r_PJ[pk], r_rope[rk]], writes=[r_tA[a], r_xs[0]])

                def P3():
                    if not is_dil or kind == 2:
                        return
                    a = u % 2
                    rk = ropek[tc]
                    tk = nxt("pj", 2)
                    emit(PE, MM(PJ[tk][:], pswap, tbs[a][:]), reads=[r_tbs[a], r_const], writes=[r_PJ[tk]])
                    emit(DVE, TT(tU, PJ[tk][:], ropeb[rk][:, 512:1024], ALU.mult),
                         reads=[r_PJ[tk], r_rope[rk]], writes=[r_tU, r_xt[0]])
                    tA0, tA1 = xs[0][0:64, a * 512:(a + 1) * 512], xs[0][64:128, a * 512:(a + 1) * 512]
                    tU0, tU1 = xt[0][0:64, 512:1024], xt[0][64:128, 512:1024]
                    if kind == 0:
                        emit(POOL, TT(QA[0:64, cs], tA0, tU0, ALU.add), reads=[r_tA[a], r_tU], writes=[rQA])
                        emit(POOL, TT(QB[64:128, cs], tA1, tU1, ALU.add), reads=[r_tA[a], r_tU], writes=[rQB])
                    else:
                        emit(POOL, TT(KA[0:64, cs], tA0, tU0, ALU.add), reads=[r_tA[a], r_tU], writes=[rKA])
                        emit(POOL, TT(KB[64:128, cs], tA1, tU1, ALU.add), reads=[r_tA[a], r_tU], writes=[rKB])

                return (P1, P2, P3)

            u = 0
            for tc in range(4):
                for kind in range(3):
                    units.append(mk_unit(tc, kind, u))
                    u += 1

            def mk_vt(h8):
                def P1():
                    tk = nxt("pj", 2)
                    for q in range(8):
                        tt = h8 * 8 + q
                        emit(PE, TP(PJb[tk][:, q * 128:(q + 1) * 128], VT[:, tt * 128:(tt + 1) * 128], ident),
                             reads=[r_VT, r_const], writes=[r_PJ[tk]])
                    emit(DVE, CP(AP(VP, h8 * 8 * 2 * VPW, [[16 * 2 * VPW, 128], [2 * VPW, 8], [VPW, 2], [1, 64]]),
                                 AP(PJb[tk], 0, [[1024, 128], [128, 8], [64, 2], [1, 64]])),
                         reads=[r_PJ[tk]], writes=[rVP])
                return (P1, None, None)

            def mk_aug(hl, which, h8):
                hidx = pp * 2 + hl
                qo, ko = (72, 76) if hl == 0 else (0, 4)
                c0, ncol, p0 = (8, 72, 64) if hl == 0 else (0, 8, 0)
                if which == 0:
                    X, dst, rdst = XQ, (QA if hl == 0 else QB), (rQA if hl == 0 else rQB)
                else:
                    X, dst, rdst = XK, (KA if hl == 0 else KB), (rKA if hl == 0 else rKB)

                def P1():
                    if h8 == 0:
                        if which == 0:
                            emit(DVE, CP(AP(XQ, qo, [[16 * XW, 128], [XW, 16], [1, 4]]),
                                         AP(P3, hidx * 4, [[768, 128], [48, 16], [1, 4]])),
                                 reads=[r_P3], writes=[r_X])
                        else:
                            emit(DVE, TS(AP(XK, ko, [[16 * XW, 128], [XW, 16], [1, 4]]),
                                         AP(P3, hidx * 4, [[768, 128], [48, 16], [1, 4]]), -1.0, ALU.mult),
                                 reads=[r_P3], writes=[r_X])
                    tk = nxt("pj", 2)
                    for q in range(8):
                        tt = h8 * 8 + q
                        emit(PE, TP(PJb[tk][0:ncol, q * 128:(q + 1) * 128],
                                    X[:, tt * XW + c0:tt * XW + c0 + ncol], ident),
                             reads=[r_X, r_const], writes=[r_PJ[tk]])
                    emit(DVE, CP(dst[p0:p0 + 8, h8 * 1024:(h8 + 1) * 1024], PJb[tk][p0:p0 + 8, 0:1024]),
                         reads=[r_PJ[tk]], writes=[rdst])
                return (P1, None, None)

            tail = [mk_vt(0), mk_vt(1)]
            if not is_dil:
                for hl in range(2):
                    for which in range(2):
                        for h8 in range(2):
                            tail.append(mk_aug(hl, which, h8))
            nu = len(units)
            for k in range(nu + 2):
                if 0 <= k - 2 < nu:
                    units[k - 2][2]()
                if 0 <= k - 1 < nu:
                    units[k - 1][1]()
                if k < nu:
                    for _ in units[k][0]():
                        yield
                yield
            for t_ in tail:
                t_[0]()
                yield

        for b in range(NB):
            mhT = yT
            jobs = [(x_d.ap()[b, tt * 128:(tt + 1) * 128, :], (g1T, big, 16384, 2048, tt * 128, r_hT[tt // 4])) for tt in range(NT)]
            jobs += [(mem_d.ap()[b, mt * 128:(mt + 1) * 128, :], (gmT, mhT, 2048, 256, mt * 128, r_yT)) for mt in range(2)]
            kprev = norm_part1(jobs[0][0])
            for ji in range(len(jobs)):
                knext = norm_part1(jobs[ji + 1][0]) if ji + 1 < len(jobs) else None
                norm_part2(kprev, *jobs[ji][1])
                kprev = knext
            def stage_c1():
                pk = nxt("pj", 2)
                for tt in range(NT):
                    for c in range(NCH):
                        emit(PE, MM(PJ[pk][:, tt * 12:(tt + 1) * 12], hT_ap(c, tt * 128, 128), wfl[:, c * 12:(c + 1) * 12],
                                    start=(c == 0), stop=(c == NCH - 1)),
                             reads=[r_const, r_hT[tt // 4]], writes=[r_PJ[pk]])
                emit(DVE, TT(fl, PJ[pk][:, 0:192], bfor[:], ALU.add), reads=[r_PJ[pk], r_const], writes=[r_sc] + r_tA)
                emit(ACT, ACTV(Lg, fl, AF.Exp, scale=-1.0), reads=[r_sc], writes=[r_sc])
                emit(ACT, ACTV(Lg, Lg, AF.Ln, bias=oneb[:, 0:1], scale=1.0), reads=[r_sc, r_zer], writes=[r_sc])

            def stage_c2():
                pk1 = nxt("pj", 2)
                emit(PE, MM(PJ[pk1][:, 0:192], negtri, Lg), reads=[r_sc, r_const], writes=[r_PJ[pk1]])
                pk2 = nxt("pj", 2)
                emit(PE, MM(PJ[pk2][:, 0:192], negones, Lg), reads=[r_sc, r_const], writes=[r_PJ[pk2]])
                emit(DVE, CP(tsb, PJ[pk2][:, 0:192]), reads=[r_PJ[pk2]], writes=[r_sc])
                emit(DVE, MS(pre[:, 0:12], 0.0), writes=[r_sc])
                for tt in range(1, NT):
                    emit(DVE, TT(pre[:, tt * 12:(tt + 1) * 12], pre[:, (tt - 1) * 12:tt * 12], tsb[:, (tt - 1) * 12:tt * 12], ALU.add),
                         reads=[r_sc], writes=[r_sc])
                emit(DVE, TT(ctm, PJ[pk1][:, 0:192], pre, ALU.add), reads=[r_PJ[pk1], r_sc], writes=[r_sc])

            def stage_c3():
                p3 = lambda t, k: AP(t, k, [[768, 128], [4, 192]])
                emit(DVE, TS(p3(P3, 0), ctm, 8.0, ALU.mult), reads=[r_sc], writes=[r_P3])
                emit(DVE, STT(fl, ctm, 8.0, p3(P3, 0), ALU.mult, ALU.subtract), reads=[r_sc, r_P3], writes=[r_sc])
                emit(DVE, CP(p3(P3, 1), fl), reads=[r_sc], writes=[r_P3])
                emit(DVE, TT(Lg, fl, p3(P3, 1), ALU.subtract), reads=[r_sc, r_P3], writes=[r_sc])
                emit(DVE, CP(p3(P3, 2), Lg), reads=[r_sc], writes=[r_P3])
                emit(DVE, TT(fl, Lg, p3(P3, 2), ALU.subtract), reads=[r_sc, r_P3], writes=[r_sc])
                emit(DVE, CP(p3(P3, 3), fl), reads=[r_sc], writes=[r_P3])


            stage_c1()
            gen0 = None
            for g8 in range(8):
                wk = load_weights(wg8_d.ap()[g8], 2048)
                for t2 in range(NT // 2):
                    pk = nxt("pj", 2)
                    for hf in range(2):
                        tt = 2 * t2 + hf
                        for c in range(NCH):
                            emit(PE, MM(PJ[pk][:, hf * 256:(hf + 1) * 256], hT_ap(c, tt * 128, 128),
                                        wsl[wk][:, c * 256:(c + 1) * 256], start=(c == 0), stop=(c == NCH - 1)),
                                 reads=[r_wsl[wk], r_hT[tt // 4]], writes=[r_PJ[pk]])
                    emit(ACT, ACTV(AP(G, 2 * t2 * 2048 + g8 * 256, [[16 * 2048, 128], [2048, 2], [1, 256]]),
                                   AP(PJ[pk], 0, [[512, 128], [256, 2], [1, 256]]), AF.Silu),
                         reads=[r_PJ[pk]],
                         writes=[r_G[2 * t2][2 * g8], r_G[2 * t2][2 * g8 + 1], r_G[2 * t2 + 1][2 * g8], r_G[2 * t2 + 1][2 * g8 + 1]])
                    if gen0 is not None:
                        for _ in range(4 if FINE else 1):
                            next(gen0, None)
                if g8 == 0:
                    stage_c2()
                elif g8 == 1:
                    stage_c3()
                elif g8 == 5:
                    gen0 = proj_pair(0, 0)

            for _ in gen0:
                pass
            MQ = [Qa[0], Qb[0], Ka[0], Kb[0]]
            r_MQ = [r_Qa[0], r_Qb[0], r_Ka[0], r_Kb[0]]

            def proj_memkv():
                for k4 in range(4):
                    wk = load_weights(wmkv4_d.ap()[k4], 2048)
                    if k4 < 2:
                        for hl in range(2):
                            hh = 2 * k4 + hl
                            pk = nxt("pj", 2)
                            for c in range(NCH):
                                emit(PE, MM(PJ[pk][:, 0:256], wsl[wk][:, c * 256 + hl * 128:c * 256 + hl * 128 + 128],
                                            mhT[:, c * 256:(c + 1) * 256], start=(c == 0), stop=(c == NCH - 1)),
                                     reads=[r_wsl[wk], r_yT], writes=[r_PJ[pk]])
                            emit(DVE, CP(MK[:, hh * 256:(hh + 1) * 256], PJ[pk][:, 0:256]),
                                 reads=[r_PJ[pk]], writes=[r_MK])
                            yield
                    else:
                        h0 = 2 * (k4 - 2)
                        for mt in range(2):
                            pk = nxt("pj", 2)
                            for c in range(NCH):
                                emit(PE, MM(PJ[pk][:, 0:256], mhT[:, c * 256 + mt * 128:c * 256 + mt * 128 + 128],
                                            wsl[wk][:, c * 256:(c + 1) * 256], start=(c == 0), stop=(c == NCH - 1)),
                                     reads=[r_wsl[wk], r_yT], writes=[r_PJ[pk]])
                            emit(DVE, CP(AP(MVp, (mt * 4 + h0) * MVW, [[2 * 4 * MVW, 128], [MVW, 2], [1, 128]]),
                                         AP(PJ[pk], 0, [[512, 128], [128, 2], [1, 128]])),
                                 reads=[r_PJ[pk]], writes=[r_MVp])
                            yield

            def proj_mem():
                pend = None
                for q2 in range(2):
                    wk = load_weights(wmq2_d.ap()[q2], 2048)
                    for hl in range(2):
                        hh = q2 * 2 + hl
                        for tc in range(4):
                            pk = nxt("pj", 2)
                            for c in range(NCH):
                                emit(PE, MM(PJ[pk][:], wsl[wk][:, c * 256 + hl * 128:c * 256 + hl * 128 + 128],
                                            hT_ap(c, tc * 512, 512), start=(c == 0), stop=(c == NCH - 1)),
                                     reads=[r_wsl[wk], r_hT[tc]], writes=[r_PJ[pk]])
                                if FINE and c % 2 == 1 and c < NCH - 1:
                                    yield
                            if pend is not None:
                                pend()
                            if tc % 2 == 0:
                                pend = (lambda hh, tc, pk: lambda: emit(
                                    DVE, CP(MQ[hh][:, tc * 512:(tc + 1) * 512], PJ[pk][:]), reads=[r_PJ[pk]], writes=[r_MQ[hh]]))(hh, tc, pk)
                            else:
                                pend = (lambda hh, tc, pk: lambda: emit(
                                    ACT, ACTV(MQ[hh][:, tc * 512:(tc + 1) * 512], PJ[pk][:], AF.Copy), reads=[r_PJ[pk]], writes=[r_MQ[hh]]))(hh, tc, pk)
                            yield
                pend()
                yield

            for pp in range(12):
                slot = pp % 2
                is_dil = pp >= 6
                nxt_gen = proj_pair(pp + 1, (pp + 1) % 2) if pp + 1 < 12 else chain_gens(proj_memkv(), proj_mem())
                for hl in range(2):
                    hidx = (pp % 6) * 2 + hl
                    qT = Qa[slot] if hl == 0 else Qb[slot]
                    kT = Ka[slot] if hl == 0 else Kb[slot]
                    r_q = r_Qa[slot] if hl == 0 else r_Qb[slot]
                    r_k = r_Ka[slot] if hl == 0 else r_Kb[slot]
                    v_of = (lambda hl, VP: lambda j: VP[:, (j * 2 + hl) * VPW:(j * 2 + hl) * VPW + 65])(hl, Vp[slot])
                    if not is_dil:
                        attention("fox", qT, kT, 0, v_of, 64, 4, hidx * 64, pp, r_q, r_k, r_Vp[slot], filler=nxt_gen)
                    else:
                        attention("dil", qT, kT, 0, v_of, 64, 4, 768 + hidx * 64, pp, r_q, r_k, r_Vp[slot], filler=nxt_gen)
                if nxt_gen is not None:
                    for _ in nxt_gen:
                        pass

            for q4 in range(4):
                emit(POOL, DMA(big[:, q4 * 4096:(q4 + 1) * 4096], wout_d.ap()[:, q4 * 4096:(q4 + 1) * 4096]),
                     writes=([r_wout] + r_hT) if q4 == 0 else [], dsem=d_wo)
            for r_ in [r_wout] + r_hT:
                r_.w[d_wo] = d_wo.n
            for hh in range(4):
                v_of = (lambda hh: lambda j: MVp[:, (j * 4 + hh) * MVW:(j * 4 + hh) * MVW + 129])(hh)
                attention("mem", MQ[hh], MK, hh * 256, v_of, 128, 2, 1536 + hh * 128, 12 + hh, r_MQ[hh], r_MK, r_MVp)
            if b + 1 < NB:
                for hh in range(4):
                    emit(POOL, MS(MQ[hh][:], 0.0), writes=[r_MQ[hh]])

            yTb = [yT, Qa[1]]
            r_yTb = [r_yT, r_Qa[1]]

            FT = [STp[0].bitcast(BF16), STp[1].bitcast(BF16), OA[0].bitcast(BF16), OA[1].bitcast(BF16)]
            r_FT = [r_ST[0], r_ST[1], r_OA[0], r_OA[1]]
            FM = [PJ[0], PJ[1], STp[2], STp[3]]
            r_FM = [r_PJ[0], r_PJ[1], r_ST[2], r_ST[3]]
            fcnt = dict(t=0, m=0)

            def f_transposes(tt):
                yk = tt % 2
                for h8 in range(2):
                    tk = fcnt["t"] % 4
                    fcnt["t"] += 1
                    for q in range(8):
                        c = h8 * 8 + q
                        emit(PE, TP(FT[tk][:, q * 128:(q + 1) * 128], G[:, tt * 2048 + c * 128:tt * 2048 + (c + 1) * 128], ident),
                             reads=[r_G[tt][c], r_const], writes=[r_FT[tk]])
                    emit(DVE, CP(yTb[yk][:, h8 * 1024:(h8 + 1) * 1024], FT[tk][:, 0:1024]),
                         reads=[r_FT[tk]], writes=[r_yTb[yk]])

            f_transposes(0)
            for tt in range(NT):
                yk = tt % 2
                k = nxt("x", 2)
                emit(SP, DMA(xt[k][:], x_d.ap()[b, tt * 128:(tt + 1) * 128, :]), writes=[r_xt[k]] + ([r_tU] if k == 0 else []), dsem=d_x[k])
                if tt + 1 < NT:
                    f_transposes(tt + 1)
                pks = []
                for half in range(2):
                    pk = fcnt["m"] % 4
                    fcnt["m"] += 1
                    pks.append(pk)
                    for c in range(16):
                        emit(PE, MM(FM[pk][:], yTb[yk][:, c * 128:(c + 1) * 128],
                                    big[:, c * 1024 + half * 512:c * 1024 + (half + 1) * 512],
                                    start=(c == 0), stop=(c == 15)),
                             reads=[r_yTb[yk], r_wout] + r_hT, writes=[r_FM[pk]])
                for half in range(2):
                    pk = pks[half]
                    emit(DVE, TT(xt[k][:, half * 512:(half + 1) * 512], FM[pk][:], xt[k][:, half * 512:(half + 1) * 512], ALU.add),
                         reads=[r_FM[pk], r_xt[k]], writes=[r_xt[k]])
                rms_stats(xt[k][:], k, r_xt[k])
                emit(DVE, STT(xs[k][:], xt[k][:], st[k][:, 2:3], gf[:], ALU.mult, ALU.mult),
                     reads=[r_xt[k], r_st[k], r_const], writes=[r_xs[k]])
                emit(SP, DMA(out_d.ap()[b, tt * 128:(tt + 1) * 128, :], xs[k][:]), reads=[r_xs[k]], dsem=d_o[k])
            if b + 1 < NB:
                emit(POOL, MS(Qa[1][:], 0.0), writes=[r_Qa[1]])

        for k in range(2):
            wait_tok(SP, d_o[k], d_o[k].n)

        for e in engines:
            e.sem = es.enter_context(nc.semaphore("sem_" + e.name))
        for d in dsems:
            d.sem = es.enter_context(nc.semaphore("dsem_" + d.name))
        with nc.Block() as block:
            @block.tensor
            def _(t):
                replay(PE, t)

            @block.scalar
            def _(a):
                replay(ACT, a)

            @block.vector
            def _(v):
                replay(DVE, v)

            @block.gpsimd
            def _(g):
                replay(POOL, g)

            @block.sync
            def _(s):
                replay(SP, s)
    return nc


def _chunked(w, ncols):
    return np.ascontiguousarray(w.reshape(8, 128, ncols).transpose(1, 0, 2).reshape(128, 8 * ncols))


def _constants():
    p = np.arange(128)
    ident = np.eye(128, dtype=np.float32)
    negmask = np.where(p[:, None] <= p[None, :], 0.0, -30000.0).astype(np.float32)
    pswap = np.zeros((128, 128), np.float32)
    for m in range(128):
        ml = m % 64
        if ml < 8:
            pswap[m + 8, m] = -1.0
        elif ml < 16:
            pswap[m - 8, m] = 1.0
    wmask = np.zeros((128, 16, 128), np.float32)
    for dlt in range(16):
        delta = 128 * dlt + p[None, :] - p[:, None]
        m1 = (delta >= 0) & (delta <= 128)
        m2 = (delta >= 0) & (delta % 4 == 0) & (delta <= 512)
        m3 = (delta >= 0) & (delta % 16 == 0) & (delta <= 2048)
        wmask[:, dlt, :] = m1.astype(np.float32) + m2 + m3
    farmask = np.where((p[None, :] - p[:, None]) % 16 == 0, 0.0, -30000.0).astype(np.float32)
    cbf = np.concatenate([ident, negmask, pswap, wmask.reshape(128, 2048), farmask], axis=1)
    negtri = -(p[:, None] <= p[None, :]).astype(np.float32)
    negones = -np.ones((128, 128), np.float32)
    cf = np.concatenate([ident, negtri, negones], axis=1)
    pos = np.arange(S, dtype=np.float32)
    inv_freq = (1.0 / (np.float32(500000.0) ** (np.arange(0, 16, 2, dtype=np.float32) / np.float32(16)))).astype(np.float32)
    ang = (pos[:, None] * inv_freq[None, :]).astype(np.float32)
    C = np.ones((128, S), np.float32)
    Sn = np.zeros((128, S), np.float32)
    for m in range(128):
        ml = m % 64
        if ml < 16:
            C[m] = np.cos(ang[:, ml % 8])
            Sn[m] = np.sin(ang[:, ml % 8])
    rope = np.concatenate([C, Sn], axis=1).astype(np.float32)
    return np.ascontiguousarray(cbf), np.ascontiguousarray(cf), np.ascontiguousarray(rope)


_PROGRAM = None


def kernel(x, mem, norm_g, w_in, b_forget, mem_norm_g, w_mem_kv, w_out, final_norm_g):
    global _PROGRAM
    x = np.asarray(x, dtype=np.float32)
    mem = np.asarray(mem, dtype=np.float32)
    w = np.asarray(w_in, dtype=np.float32)[0]
    sizes = [768] * 4 + [12] + [768] * 4 + [512] * 2
    offs = np.cumsum([0] + sizes)
    fq, fk, fv, fg, flg, dq, dk, dv, dg, mq, mg = [w[:, offs[i]:offs[i + 1]] for i in range(11)]
    wpair = np.zeros((12, 128, WSLOT), np.float32)
    for pp in range(6):
        cs = slice(pp * 128, (pp + 1) * 128)
        wpair[pp] = _chunked(np.concatenate([fq[:, cs], fk[:, cs], fv[:, cs]], axis=1), 384)
        wpair[6 + pp] = _chunked(np.concatenate([dq[:, cs], dk[:, cs], dv[:, cs]], axis=1), 384)
    gates = np.concatenate([fg, dg, mg], axis=1)
    wg8 = np.stack([_chunked(gates[:, i * 256:(i + 1) * 256], 256) for i in range(8)])
    wmq2 = np.stack([_chunked(mq[:, i * 256:(i + 1) * 256], 256) for i in range(2)])
    wkv = np.asarray(w_mem_kv, dtype=np.float32)[0]
    wmkv4 = np.stack([_chunked(wkv[:, i * 256:(i + 1) * 256], 256) for i in range(4)])
    wfl = _chunked(flg, 12)
    wo = np.asarray(w_out, dtype=np.float32)[0]
    wout = np.ascontiguousarray(wo.reshape(16, 128, 1024).transpose(1, 0, 2).reshape(128, 16384))
    gfb = np.ascontiguousarray(np.broadcast_to(np.asarray(final_norm_g, np.float32)[None, :], (128, 1024)))
    g1T = np.ascontiguousarray(np.asarray(norm_g, np.float32)[0].reshape(8, 128).T)
    gmT = np.ascontiguousarray(np.asarray(mem_norm_g, np.float32)[0].reshape(8, 128).T)
    bfor = np.ascontiguousarray(np.broadcast_to(np.asarray(b_forget, np.float32)[0][None, None, :], (128, 16, 12)).reshape(128, 192))
    cbf, cf, rope = _constants()

    if _PROGRAM is None:
        _PROGRAM = build_program()
    nc = _PROGRAM
    shared = dict(wpair=wpair, wg8=wg8, wmq2=wmq2, wmkv4=wmkv4, wfl=wfl, wout=wout, gf=gfb, g1T=g1T, gmT=gmT,
                  bfor=bfor, cbf=cbf, cf=cf, rope=rope)
    in_maps = []
    for c in range(NCORES):
        m = dict(shared)
        m["x"] = np.ascontiguousarray(x[c * NB:(c + 1) * NB])
        m["mem"] = np.ascontiguousarray(mem[c * NB:(c + 1) * NB])
        in_maps.append(m)
    res = run_bass_kernel_spmd(nc, in_maps, core_ids=list(range(NCORES)))
    out = np.concatenate([np.asarray(r["out"], dtype=np.float32) for r in res.results], axis=0)
    return out
```
